# Optimizing a Trainium2 kernel written in Bass

```python
import math
import jax, jax.numpy as jnp
from jax import lax
import numpy as np

D_MODEL = 1024
BATCH = 4
SEQ = 4096
DEPTH = 4

CHUNK = 64
N_MEM = 256
EPS = 1e-6
NEG_INF = -1e30
N_NORMS = 6

BRANCH_DIM = D_MODEL // 2
N_BRANCH = 3

HEAD_DIM = 64
A_Q_HEADS = BRANCH_DIM // HEAD_DIM
A_KV_HEADS = 2
A_GROUP = A_Q_HEADS // A_KV_HEADS
A_Q_DIM = A_Q_HEADS * HEAD_DIM
A_KV_DIM = A_KV_HEADS * HEAD_DIM
WINDOW = 128
WINDOW_CHUNKS = WINDOW // CHUNK
ATT_BLOCK = 128

SGU_CHUNK = 128
SGU_GROUPS = 4
SGU_DIM = BRANCH_DIM
SGU_GROUP_DIM = SGU_DIM // SGU_GROUPS

POOL_WINDOWS = (2, 4, 8, 16)
POOL_GROUPS = 4
POOL_DIM = BRANCH_DIM
POOL_GROUP_DIM = POOL_DIM // POOL_GROUPS

IN_SIZES = (A_Q_DIM, A_KV_DIM, A_KV_DIM, SGU_DIM, SGU_DIM, POOL_DIM, N_BRANCH * D_MODEL)
IN_DIM = A_Q_DIM + 2 * A_KV_DIM + 2 * SGU_DIM + POOL_DIM + N_BRANCH * D_MODEL

MEM_HEADS = 4
MEM_HEAD_DIM = 128
MEM_DIM = MEM_HEADS * MEM_HEAD_DIM

D_FF = 4 * D_MODEL

kernel_name = "hybrid_gated_parallel_streaming_trunk"


def rmsnorm(x, g):
    xf = x.astype(jnp.float32)
    y = xf * lax.rsqrt(jnp.mean(xf * xf, axis=-1, keepdims=True) + EPS)
    return (y * g.astype(jnp.float32)).astype(x.dtype)


def split_columns(proj):
    parts, start = [], 0
    for size in IN_SIZES:
        parts.append(proj[..., start:start + size])
        start += size
    return parts


def window_sink_attention(q, k, v, sinks):
    B, S, _ = q.shape
    nb = S // ATT_BLOCK
    qb = q.reshape(B, nb, ATT_BLOCK, A_KV_HEADS, A_GROUP, HEAD_DIM)

    def band(t):
        t = t.reshape(B, S, A_KV_HEADS, HEAD_DIM)
        t = jnp.pad(t, ((0, 0), (ATT_BLOCK, 0), (0, 0), (0, 0)))
        t = t.reshape(B, nb + 1, ATT_BLOCK, A_KV_HEADS, HEAD_DIM)
        return jnp.concatenate([t[:, :-1], t[:, 1:]], axis=2)

    kb, vb = band(k), band(v)
    s = jnp.einsum('bnqhgd,bnkhd->bnhgqk', qb, kb).astype(jnp.float32) * (1.0 / math.sqrt(HEAD_DIM))

    blk = jnp.arange(nb)[:, None, None]
    qpos = blk * ATT_BLOCK + jnp.arange(ATT_BLOCK)[None, :, None]
    kpos = (blk - 1) * ATT_BLOCK + jnp.arange(2 * ATT_BLOCK)[None, None, :]
    qc, kc = qpos // CHUNK, kpos // CHUNK
    valid = (kpos >= 0) & (kc <= qc) & (kc >= qc - WINDOW_CHUNKS)
    s = jnp.where(valid[None, :, None, None], s, NEG_INF)

    sink = sinks.astype(jnp.float32).reshape(A_KV_HEADS, A_GROUP)[None, None, :, :, None, None]
    m = jnp.maximum(jnp.max(s, axis=-1, keepdims=True), sink)
    p = jnp.exp(s - m)
    p = p / (jnp.sum(p, axis=-1, keepdims=True) + jnp.exp(sink - m))
    o = jnp.einsum('bnhgqk,bnkhd->bnqhgd', p.astype(v.dtype), vb)
    return o.reshape(B, S, A_Q_DIM)


def spatial_gating(u, v, g_sgu, w_s, b_s):
    B, S, _ = u.shape
    nc = S // SGU_CHUNK
    u = jax.nn.gelu(u)
    v = rmsnorm(jax.nn.gelu(v), g_sgu)
    vb = v.reshape(B, nc, SGU_CHUNK, SGU_GROUPS, SGU_GROUP_DIM)
    pc = jnp.arange(SGU_CHUNK) // CHUNK
    mask = pc[None, :] <= pc[:, None]
    w = jnp.where(mask[None], w_s, 0.0).astype(v.dtype)
    sp = jnp.einsum('gij,bnjgc->bnigc', w, vb) + b_s.T[:, :, None].astype(v.dtype)
    return u * sp.reshape(B, S, SGU_DIM)


def multiscale_pool(c, w_pool, pool_scale):
    B, S, _ = c.shape
    cs = jnp.cumsum(c.astype(jnp.float32), axis=1)
    t = jnp.arange(S)
    outs = []
    for gi, w in enumerate(POOL_WINDOWS):
        cg = cs[..., gi * POOL_GROUP_DIM:(gi + 1) * POOL_GROUP_DIM]
        lag = jnp.pad(cg, ((0, 0), (w, 0), (0, 0)))[:, :S]
        cnt = jnp.minimum(t + 1, w).astype(jnp.float32)[None, :, None]
        outs.append((cg - lag) / cnt)
    pooled = jnp.concatenate(outs, axis=-1).astype(c.dtype) - c
    pooled = pooled.reshape(B, S, POOL_GROUPS, POOL_GROUP_DIM)
    mixed = jnp.einsum('bsgc,gcd->bsgd', pooled, w_pool).reshape(B, S, POOL_DIM)
    return mixed * pool_scale


def memory_attention(h, mem_n, w_q, w_kv, w_o):
    B, S, _ = h.shape
    q = (h @ w_q).reshape(B, S, MEM_HEADS, MEM_HEAD_DIM)
    kv = mem_n @ w_kv
    k = kv[..., :MEM_DIM].reshape(B, N_MEM, MEM_HEADS, MEM_HEAD_DIM)
    v = kv[..., MEM_DIM:].reshape(B, N_MEM, MEM_HEADS, MEM_HEAD_DIM)
    s = jnp.einsum('bshd,bmhd->bhsm', q, k).astype(jnp.float32) * (1.0 / math.sqrt(MEM_HEAD_DIM))
    p = jax.nn.softmax(s, axis=-1).astype(h.dtype)
    o = jnp.einsum('bhsm,bmhd->bshd', p, v).reshape(B, S, MEM_DIM)
    return o @ w_o


def setup_inputs(seed: int = 0) -> dict:
    key = jax.random.key(seed)
    ks = jax.random.split(key, 20)
    f32 = jnp.float32

    def dense(k, shape, fan_in):
        return jax.random.normal(k, shape, f32) * (fan_in ** -0.5)

    return {
        "x": jax.random.normal(ks[0], (BATCH, SEQ, D_MODEL), f32),
        "mem": jax.random.normal(ks[1], (BATCH, N_MEM, D_MODEL), f32),
        "g_norm": 1.0 + 0.1 * jax.random.normal(ks[2], (DEPTH, N_NORMS, D_MODEL), f32),
        "g_mem": 1.0 + 0.1 * jax.random.normal(ks[3], (DEPTH, D_MODEL), f32),
        "w_in": dense(ks[4], (DEPTH, D_MODEL, IN_DIM), D_MODEL),
        "attn_sinks": 0.5 * jax.random.normal(ks[5], (DEPTH, A_Q_HEADS), f32),
        "w_spatial": dense(ks[6], (DEPTH, SGU_GROUPS, SGU_CHUNK, SGU_CHUNK), SGU_CHUNK),
        "b_spatial": 1.0 + 0.1 * jax.random.normal(ks[7], (DEPTH, SGU_GROUPS, SGU_CHUNK), f32),
        "g_sgu": 1.0 + 0.1 * jax.random.normal(ks[8], (DEPTH, SGU_DIM), f32),
        "w_pool": dense(ks[9], (DEPTH, POOL_GROUPS, POOL_GROUP_DIM, POOL_GROUP_DIM), POOL_GROUP_DIM),
        "pool_scale": 1.0 + 0.1 * jax.random.normal(ks[10], (DEPTH, POOL_DIM), f32),
        "w_branch": dense(ks[11], (DEPTH, N_BRANCH, BRANCH_DIM, D_MODEL), BRANCH_DIM),
        "w_out": dense(ks[12], (DEPTH, D_MODEL, D_MODEL), D_MODEL),
        "w_q_mem": dense(ks[13], (DEPTH, D_MODEL, MEM_DIM), D_MODEL),
        "w_kv_mem": dense(ks[14], (DEPTH, D_MODEL, 2 * MEM_DIM), D_MODEL),
        "w_o_mem": dense(ks[15], (DEPTH, MEM_DIM, D_MODEL), MEM_DIM),
        "w_up": dense(ks[16], (DEPTH, D_MODEL, D_FF), D_MODEL),
        "w_down": dense(ks[17], (DEPTH, D_FF, D_MODEL), D_FF),
    }


def reference(x, mem, g_norm, g_mem, w_in, attn_sinks, w_spatial, b_spatial, g_sgu,
              w_pool, pool_scale, w_branch, w_out, w_q_mem, w_kv_mem, w_o_mem, w_up, w_down):
    B, S, _ = x.shape
    for l in range(DEPTH):
        h = rmsnorm(x, g_norm[l, 0])
        q, k, v, su, sv, pc, gate = split_columns(h @ w_in[l])
        ya = window_sink_attention(q, k, v, attn_sinks[l])
        yb = spatial_gating(su, sv, g_sgu[l], w_spatial[l], b_spatial[l])
        yc = multiscale_pool(pc, w_pool[l], pool_scale[l])
        branches = jnp.stack([ya, yb, yc], axis=2)
        proj = jnp.einsum('bsnc,ncd->bsnd', branches, w_branch[l])
        gates = jax.nn.sigmoid(gate.reshape(B, S, N_BRANCH, D_MODEL))
        merged = jnp.sum(gates * proj, axis=2)
        x = x + rmsnorm(merged @ w_out[l], g_norm[l, 1])
        hm = rmsnorm(x, g_norm[l, 2])
        mem_n = rmsnorm(mem, g_mem[l])
        ym = memory_attention(hm, mem_n, w_q_mem[l], w_kv_mem[l], w_o_mem[l])
        x = x + rmsnorm(ym, g_norm[l, 3])
        hf = rmsnorm(x, g_norm[l, 4])
        yf = jnp.square(jax.nn.relu(hf @ w_up[l])) @ w_down[l]
        x = x + rmsnorm(yf, g_norm[l, 5])
    return x
```

```python
import contextlib
import math
import numpy as np
import concourse.bass as bass
import concourse.mybir as mybir
from concourse.alu_op_type import AluOpType as ALU
from concourse.bass_utils import run_bass_kernel_spmd

F32 = mybir.dt.float32
BF16 = mybir.dt.bfloat16
AF = mybir.ActivationFunctionType

D = 1024
SEQ = 4096
BATCH = 4
DEPTH = 4
NCORES = 8
TOK = 2048
HALO = 512
TT = TOK + HALO
NBLK = TT // 128
IN_DIM = 5376
EPS = 1e-6
SLOT = 2048
RING = 5
NV = 64
NBC = 1024

COMPUTE = ('pe', 'act', 'dve', 'pool')
QUEUES = ('pe', 'act', 'dve', 'pool', 'sp')


def I(method, *args, **kwargs):
    return lambda h: getattr(h, method)(*args, **kwargs)


class Op:
    __slots__ = ('eng', 'fn', 'deps', 'dma', 'waits', 'sig', 'sigval', 'dsem', 'dval', 'idx')


class Sched:
    def __init__(self):
        self.ops = []
        self.last_w = {}
        self.readers = {}
        self.dma_sems = {}

    def add(self, eng, fn, reads=(), writes=(), dma=None):
        op = Op()
        op.eng, op.fn, op.dma = eng, fn, dma
        op.idx = len(self.ops)
        op.sig = False
        op.sigval = None
        op.waits = []
        if dma is not None:
            n = self.dma_sems.get(dma, 0) + 1
            self.dma_sems[dma] = n
            op.dsem, op.dval = dma, 16 * n
        deps = {}
        for r in reads:
            w = self.last_w.get(r)
            if w is not None:
                deps[w.idx] = w
            if isinstance(r, tuple) and r[0] == 'ps':
                for o in self.readers.get(r, ()):
                    if o.eng != eng:
                        deps[o.idx] = o
        for wkey in writes:
            w = self.last_w.get(wkey)
            if w is not None:
                deps[w.idx] = w
            for o in self.readers.get(wkey, ()):
                deps[o.idx] = o
        fdeps = []
        for d in deps.values():
            if d is op:
                continue
            if d.dma is None and dma is None and d.eng == eng and eng == 'pe':
                continue
            fdeps.append(d)
        op.deps = fdeps
        for wkey in writes:
            self.last_w[wkey] = op
            self.readers[wkey] = []
        for r in reads:
            lst = self.readers.setdefault(r, [])
            if dma is None:
                lst[:] = [o for o in lst if not (o.dma is None and o.eng == eng)]
            lst.append(op)
        self.ops.append(op)
        return op

    def finalize(self):
        for op in self.ops:
            for d in op.deps:
                if d.dma is None:
                    d.sig = True
        cnt = {e: 0 for e in QUEUES}
        for op in self.ops:
            if op.dma is None and op.sig:
                cnt[op.eng] += 1
                op.sigval = cnt[op.eng]
        waited = {e: {} for e in QUEUES}
        for op in self.ops:
            need = {}
            for d in op.deps:
                if d.dma is not None:
                    key, val = 'dma:' + d.dsem, d.dval
                else:
                    key, val = d.eng, d.sigval
                if val > need.get(key, 0):
                    need[key] = val
            wq = waited[op.eng]
            for key, val in need.items():
                if wq.get(key, 0) >= val:
                    continue
                wq[key] = val
                op.waits.append((key, val))

    def emit(self, nc):
        self.finalize()
        with contextlib.ExitStack() as st:
            sems = {}
            for e in QUEUES:
                sems[e] = st.enter_context(nc.semaphore('s_' + e))
            for name in self.dma_sems:
                sems['dma:' + name] = st.enter_context(nc.semaphore('d_' + name))
            block = st.enter_context(nc.Block())
            per = {e: [op for op in self.ops if op.eng == e] for e in QUEUES}

            def run(handle, ename):
                for op in per[ename]:
                    for key, val in op.waits:
                        handle.wait_ge(sems[key], val)
                    ins = op.fn(handle)
                    if op.dma is not None:
                        ins.then_inc(sems['dma:' + op.dsem], 16)
                    elif op.sig:
                        ins.then_inc(sems[ename], 1)

            @block.tensor
            def _(h):
                run(h, 'pe')

            @block.scalar
            def _(h):
                run(h, 'act')

            @block.vector
            def _(h):
                run(h, 'dve')

            @block.gpsimd
            def _(h):
                run(h, 'pool')

            @block.sync
            def _(h):
                run(h, 'sp')


def layer_chunks():
    ch = [('q0', 2048), ('q1', 2048), ('k', 2048), ('v', 1024),
          ('su0', 2048), ('su1', 2048), ('sv0', 2048), ('sv1', 2048), ('pc0', 2048), ('pc1', 2048),
          ('ws', 512), ('wp', 512)]
    for dc in range(8):
        for n in range(3):
            ch.append(('gb%d_%d' % (dc, n), 1536))
    for i in range(4):
        ch.append(('wo%d' % i, 2048))
    ch += [('wq0', 2048), ('wq1', 2048), ('wom0', 2048), ('wom1', 2048)]
    for i in range(16):
        ch.append(('up%d' % i, 2048))
    for dc in range(8):
        ch.append(('dn%da' % dc, 2048))
        ch.append(('dn%db' % dc, 2048))
    return ch


def mem_chunks():
    return [('mk0', 2048), ('mk1', 2048), ('mv0', 2048), ('mv1', 2048)]


def chunk_offsets():
    offs = {}
    o = 0
    for name, size in mem_chunks() + layer_chunks():
        offs[name] = (o, size)
        o += size
    return offs, o


def _kblk(W, cols):
    K = W.shape[0]
    Wc = W[:, cols]
    return np.ascontiguousarray(Wc.reshape(K // 128, 128, Wc.shape[1]).transpose(1, 0, 2)).reshape(128, -1)


def pack_layer_weights(l, w_in, w_spatial, w_pool, w_branch, w_out, w_q_mem, w_kv_mem, w_o_mem, w_up, w_down):
    offs, tot = chunk_offsets()
    out = np.zeros((128, tot), np.float32)

    def put(name, arr):
        o, size = offs[name]
        assert arr.shape == (128, size), (name, arr.shape, size)
        out[:, o:o + size] = arr

    Wi = w_in[l]
    ar = np.arange
    put('q0', _kblk(Wi, ar(0, 256)))
    put('q1', _kblk(Wi, ar(256, 512)))
    kcols = np.concatenate([ar(512, 576), ar(512, 576), ar(576, 640), ar(576, 640)])
    put('k', _kblk(Wi, kcols))
    put('v', _kblk(Wi, ar(640, 768)))
    put('su0', _kblk(Wi, ar(768, 1024)))
    put('su1', _kblk(Wi, ar(1024, 1280)))
    put('sv0', _kblk(Wi, ar(1280, 1536)))
    put('sv1', _kblk(Wi, ar(1536, 1792)))
    put('pc0', _kblk(Wi, ar(1792, 2048)))
    put('pc1', _kblk(Wi, ar(2048, 2304)))
    put('ws', np.ascontiguousarray(w_spatial[l].transpose(2, 0, 1)).reshape(128, 512))
    put('wp', np.ascontiguousarray(w_pool[l].transpose(1, 0, 2)).reshape(128, 512))
    for dc in range(8):
        for n in range(3):
            g = _kblk(Wi, ar(2304 + n * 1024 + dc * 128, 2304 + n * 1024 + (dc + 1) * 128))
            b = _kblk(w_branch[l, n], ar(dc * 128, (dc + 1) * 128))
            put('gb%d_%d' % (dc, n), np.concatenate([g, b], axis=1))
    for i in range(4):
        put('wo%d' % i, _kblk(w_out[l], ar(i * 256, (i + 1) * 256)))
    put('wq0', _kblk(w_q_mem[l], ar(0, 256)))
    put('wq1', _kblk(w_q_mem[l], ar(256, 512)))
    put('wom0', _kblk(w_o_mem[l], ar(0, 512)))
    put('wom1', _kblk(w_o_mem[l], ar(512, 1024)))
    for i in range(16):
        put('up%d' % i, _kblk(w_up[l], ar(i * 256, (i + 1) * 256)))
    for dc in range(8):
        full = _kblk(w_down[l], ar(dc * 128, (dc + 1) * 128)).reshape(128, 32, 128)
        put('dn%da' % dc, full[:, 0:16, :].reshape(128, 2048))
        put('dn%db' % dc, full[:, 16:32, :].reshape(128, 2048))
    Wkv = w_kv_mem[l]
    put('mk0', _kblk(Wkv, ar(0, 256)))
    put('mk1', _kblk(Wkv, ar(256, 512)))
    put('mv0', _kblk(Wkv, ar(512, 768)))
    put('mv1', _kblk(Wkv, ar(768, 1024)))
    return out


def build_program(layers, first_blocks, max_sbs=None):
    nL = len(layers)
    offs, WTOT = chunk_offsets()
    nc = bass.Bass("TRN2", target_bir_lowering=False)
    x_in = nc.dram_tensor("xT_in", [128, 8, TT], F32, kind="ExternalInput").ap()
    mem_in = nc.dram_tensor("memT", [128, 8, 256], F32, kind="ExternalInput").ap()
    wst = nc.dram_tensor("wst", [nL, 128, WTOT], F32, kind="ExternalInput").ap()
    vecs_in = nc.dram_tensor("vecs", [128, nL * NV], F32, kind="ExternalInput").ap()
    bc_in = nc.dram_tensor("bc", [nL, 128, NBC], F32, kind="ExternalInput").ap()
    valid_in = nc.dram_tensor("valid", [128, NBLK], F32, kind="ExternalInput").ap()
    ptm_in = nc.dram_tensor("ptm", [128, 2 * 4 * 2 * 128], F32, kind="ExternalInput").ap()
    out_d = nc.dram_tensor("outT", [128, 8, TOK], F32, kind="ExternalOutput").ap()

    S = Sched()
    st = contextlib.ExitStack()
    with st:
        def sb(name, shape, dt):
            return st.enter_context(nc.sbuf_tensor(name, shape, dt))

        xT = sb("xT", [128, 8, TT], F32)
        hT = sb("hT", [128, 8, 512], BF16)
        B = [sb("B%d" % i, [128, 4, 512], BF16) for i in range(8)]
        ybuf = sb("ybuf", [128, 8, 512], F32)
        T = [sb("T%d" % i, [128, 512], F32) for i in range(5)]
        rstdT = sb("rstdT", [128, 512], F32)
        sq = sb("sq", [128, 2, 512], BF16)
        ptb = sb("ptb", [128, 4, 512], BF16)
        kTe = sb("kTe", [128, 2, 5 * 128], BF16)
        kTo = sb("kTo", [128, 2, 5 * 128], BF16)
        v_sb = sb("v_sb", [128, 5, 128], BF16)
        vrep = sb("vrep", [128, NBLK, 64], BF16)
        c_sb = sb("c_sb", [128, 5, 512], BF16)
        kmT = sb("kmT", [128, 4, 256], BF16)
        vmem = sb("vmem", [128, 2, 512], BF16)
        ring = [sb("ring%d" % i, [128, SLOT], BF16) for i in range(RING)]
        vecs = sb("vecs_sb", [128, nL * NV], F32)
        bcs = sb("bc_sb", [128, NBC], F32)
        valid = sb("valid_sb", [128, NBLK], F32)
        ptm = sb("ptm_sb", [128, 2 * 4 * 2 * 128], BF16)
        ones_m = sb("ones_m", [128, 128], BF16)
        ones_1 = sb("ones_1", [128, 128], BF16)
        onesf = sb("onesf", [128, 128], F32)
        negh = sb("negh", [128, 512], F32)
        esb = sb("esb", [128, 4, 128], F32)
        small = sb("small", [128, 8], F32)
        ps = [st.enter_context(nc.psum_tensor("ps%d" % i, [128, 512], F32)) for i in range(8)]

        ptm_v = ptm[:, :].rearrange("p (v g o t) -> p v g o t", v=2, g=4, o=2)

        seq = []
        for li in range(nL):
            fb = first_blocks[li]
            sbs = [sb_ for sb_ in superblocks(fb) if max_sbs is None or sb_[0] < 4 * max_sbs]
            for si in range(len(sbs)):
                if si == 0:
                    seq += [(li, n, s) for n, s in mem_chunks()]
                if sbs[si][2]:
                    seq += [(li, n, s) for n, s in layer_chunks() if n in KV_CHUNKS]
                else:
                    seq += [(li, n, s) for n, s in layer_chunks()]
        wstate = {'issued': 0, 'acq': 0}

        def w_issue():
            i = wstate['issued']
            if i >= len(seq):
                return
            li, name, size = seq[i]
            o, _ = offs[name]
            slot = i % RING
            S.add('pool', I('dma_start',
                out=ring[slot][:, 0:size], in_=wst[li, :, o:o + size]),
                writes=[('w', slot)], dma='w%d' % slot)
            wstate['issued'] = i + 1

        def w_acquire(name, li):
            i = wstate['acq']
            assert i < wstate['issued'], "weight ring deadlock: acquire before issue (%s)" % name
            assert seq[i][0] == li and seq[i][1] == name, (seq[i], li, name)
            wstate['acq'] = i + 1
            slot = i % RING
            return ring[slot], ('w', slot)

        def w_release():
            w_issue()

        rot = {'g': 0, 's': 0}

        def bank():
            b = rot['g'] % 6
            rot['g'] += 1
            return b

        def sbank():
            b = 6 + rot['s'] % 2
            rot['s'] += 1
            return b

        trot = {'i': 0}

        def tmp():
            i = trot['i'] % 5
            trot['i'] += 1
            return i

        S.add('sp', I('dma_start', out=vecs[:, :], in_=vecs_in), writes=['vecs'], dma='c0')
        S.add('sp', I('dma_start', out=valid[:, :], in_=valid_in), writes=['valid'], dma='c1')
        S.add('pool', I('dma_start', out=ptm[:, :], in_=ptm_in), writes=['ptm'], dma='c2')
        nsb_x = TT // 512 if max_sbs is None else min(TT // 512, max_sbs)
        for i in range(nsb_x):
            S.add('sp', I('dma_start', out=xT[:, :, i * 512:(i + 1) * 512],
                                                   in_=x_in[:, :, i * 512:(i + 1) * 512]),
                  writes=[('x', b, d) for b in range(4 * i, 4 * i + 4) for d in range(8)], dma='x%d' % i)
        for _ in range(RING):
            w_issue()
        S.add('dve', I('memset', ones_m[:, :], 1.0 / 1024.0), writes=['ones_m'])
        S.add('dve', I('memset', ones_1[:, :], 1.0), writes=['ones_1'])
        S.add('dve', I('memset', onesf[:, :], 1.0), writes=['onesf'])
        S.add('dve', I('memset', negh[:, :], -0.5), writes=['negh'])
        S.add('dve', I('memset', ptb[:, :, :], 0.0), writes=[('ptb', i) for i in range(4)])
        S.add('dve', I('memset', kTe[:, :, :], 0.0), writes=[('kT', g, s5) for g in range(2) for s5 in range(5)])
        S.add('dve', I('memset', kTo[:, :, :], 0.0), writes=[('kT', g, s5) for g in range(2) for s5 in range(5)])
        for b in range(NBLK):
            S.add('dve', I('tensor_scalar', out=vrep[:, b, :], in0=onesf[:, 0:64], scalar1=valid[:, b:b + 1],
                                                       scalar2=None, op0=ALU.mult),
                  reads=['onesf', 'valid'], writes=[('vrep', b)])

        def xres(blocks, dc=None):
            if dc is None:
                return [('x', b, d) for b in blocks for d in range(8)]
            return [('x', b, dc) for b in blocks]

        def rstd_from(ssb, N):
            ta = tmp()
            S.add('dve', I('tensor_scalar', out=T[ta][:, :N], in0=ps[ssb][:, :N], scalar1=EPS, scalar2=None,
                                                   op0=ALU.add),
                  reads=[('ps', ssb)], writes=[('T', ta)])
            tb = tmp()
            S.add('act', I('activation', out=T[tb][:, :N], in_=T[ta][:, :N], func=AF.Ln),
                  reads=[('T', ta)], writes=[('T', tb)])
            S.add('act', I('activation', out=rstdT[:, :N], in_=T[tb][:, :N], func=AF.Exp, scale=-0.5),
                  reads=[('T', tb)], writes=['rstd'])
            return None

        def norm_in(li, gi, blocks, t0, N):
            ssb = sbank()
            for dc in range(8):
                S.add('act', I('activation', out=sq[:, dc % 2, :N], in_=xT[:, dc, t0:t0 + N], func=AF.Square),
                      reads=xres(blocks, dc), writes=[('sq', dc % 2)])
                S.add('pe', I('matmul', ps[ssb][:, :N], lhsT=ones_m[:, :], rhs=sq[:, dc % 2, :N],
                                                      start=(dc == 0), stop=(dc == 7)),
                      reads=[('sq', dc % 2), 'ones_m'], writes=[('ps', ssb)])
            rb = rstd_from(ssb, N)
            gcol = li * NV + gi * 8
            for dc in range(8):
                S.add('dve', I('scalar_tensor_tensor',
                    out=hT[:, dc, :N], in0=xT[:, dc, t0:t0 + N], scalar=vecs[:, gcol + dc:gcol + dc + 1],
                    in1=rstdT[:, :N], op0=ALU.mult, op1=ALU.mult),
                    reads=xres(blocks, dc) + ['rstd', 'vecs'], writes=[('hT', dc)])

        class Epilogue:
            def __init__(self, li, gi, blocks, t0, N, yscale):
                self.li, self.gi, self.blocks, self.t0, self.N, self.ys = li, gi, blocks, t0, N, yscale
                self.ssb = sbank()
                self.pending = None
                self.gcol = li * NV + gi * 8

            def push(self, dc, bk):
                N, ys = self.N, self.ys
                gc = self.gcol + dc
                S.add('act', I('activation', out=sq[:, dc % 2, :N], in_=ps[bk][:, :N], func=AF.Square, scale=ys),
                      reads=[('ps', bk)], writes=[('sq', dc % 2)])
                S.add('dve', I('tensor_scalar', out=ybuf[:, dc, :N], in0=ps[bk][:, :N], scalar1=vecs[:, gc:gc + 1],
                                                       scalar2=ys, op0=ALU.mult, op1=ALU.mult),
                      reads=[('ps', bk), 'vecs'], writes=[('ybuf', dc)])
                self.flush()
                self.pending = dc

            def flush(self):
                if self.pending is None:
                    return
                dc, N, ssb = self.pending, self.N, self.ssb
                S.add('pe', I('matmul', ps[ssb][:, :N], lhsT=ones_m[:, :], rhs=sq[:, dc % 2, :N],
                                               start=(dc == 0), stop=(dc == 7)),
                      reads=[('sq', dc % 2), 'ones_m'], writes=[('ps', ssb)])
                self.pending = None

            def finish(self):
                self.flush()
                N, t0 = self.N, self.t0
                rb = rstd_from(self.ssb, N)
                xr = xres(self.blocks)
                for dc in (0, 4, 1, 5, 2, 6, 3, 7):
                    tc_ = tmp()
                    eng = 'dve' if dc < 4 else 'pool'
                    S.add(eng, I('tensor_tensor', out=T[tc_][:, :N], in0=ybuf[:, dc, :N],
                                                                          in1=rstdT[:, :N], op=ALU.mult),
                          reads=[('ybuf', dc), 'rstd'], writes=[('T', tc_)])
                    S.add(eng, I('tensor_tensor', out=xT[:, dc, t0:t0 + N],
                                                                           in0=xT[:, dc, t0:t0 + N],
                                                                           in1=T[tc_][:, :N], op=ALU.add),
                          reads=xres(self.blocks, dc) + [('T', tc_)], writes=xres(self.blocks, dc))

        def lin_fm(bk, wv, wres, KC, rhs_fn, rhs_res_fn, N, M=128):
            for kc in range(KC):
                S.add('pe', I('matmul', ps[bk][0:M, :N], lhsT=wv(kc), rhs=rhs_fn(kc),
                                                      start=(kc == 0), stop=(kc == KC - 1)),
                      reads=[wres] + rhs_res_fn(kc), writes=[('ps', bk)])

        def gelu2(src_ap, res_src, N, out_fn):
            ta = tmp()
            S.add('act', I('activation', out=T[ta][:, :N], in_=src_ap, func=AF.Square,
                                                scale=math.sqrt(0.044715)),
                  reads=res_src, writes=[('T', ta)])
            tb = tmp()
            S.add('dve', I('scalar_tensor_tensor', out=T[tb][:, :N], in0=T[ta][:, :N], scalar=1.0, in1=src_ap,
                                                          op0=ALU.add, op1=ALU.mult),
                  reads=res_src + [('T', ta)], writes=[('T', tb)])
            tcc = tmp()
            S.add('act', I('activation', out=T[tcc][:, :N], in_=T[tb][:, :N], func=AF.Tanh,
                                                scale=math.sqrt(2.0 / math.pi)),
                  reads=[('T', tb)], writes=[('T', tcc)])
            out_fn(tcc)

        for li in range(nL):
            fb = first_blocks[li]
            sbs = [sb_ for sb_ in superblocks(fb) if max_sbs is None or sb_[0] < 4 * max_sbs]
            vb0 = li * NV
            S.add('sp', I('dma_start', out=bcs[:, :], in_=bc_in[li]), writes=['bc'], dma='bc')
            for c in range(4):
                S.add('act', I('activation', out=esb[:, c, :], in_=onesf[:, :], func=AF.Exp, scale=0.0,
                                                         bias=vecs[:, vb0 + 60 + c:vb0 + 61 + c]),
                      reads=['onesf', 'vecs'], writes=['esb'])

            S.add('sp', I('dma_start', out=ybuf[:, :, 0:256], in_=mem_in),
                  writes=[('ybuf', dc) for dc in range(8)], dma='mem')
            ssb = sbank()
            for dc in range(8):
                S.add('act', I('activation', out=sq[:, dc % 2, 0:256], in_=ybuf[:, dc, 0:256], func=AF.Square),
                      reads=[('ybuf', dc)], writes=[('sq', dc % 2)])
                S.add('pe', I('matmul', ps[ssb][:, 0:256], lhsT=ones_m[:, :], rhs=sq[:, dc % 2, 0:256],
                                                               start=(dc == 0), stop=(dc == 7)),
                      reads=[('sq', dc % 2), 'ones_m'], writes=[('ps', ssb)])
            rb = rstd_from(ssb, 256)
            for dc in range(8):
                gc = vb0 + 52 + dc
                S.add('dve', I('scalar_tensor_tensor',
                    out=hT[:, dc, 0:256], in0=ybuf[:, dc, 0:256], scalar=vecs[:, gc:gc + 1], in1=rstdT[:, 0:256],
                    op0=ALU.mult, op1=ALU.mult),
                    reads=[('ybuf', dc), 'rstd', 'vecs'], writes=[('hT', dc)])
            for ci in range(2):
                wt, wres = w_acquire('mk%d' % ci, li)
                wv3 = wt[:, 0:2048].rearrange("p (k m) -> p k m", k=8)
                for hh in range(2):
                    head = ci * 2 + hh
                    bk = bank()
                    lin_fm(bk, lambda kc, hh=hh, wv3=wv3: wv3[:, kc, hh * 128:(hh + 1) * 128], wres, 8,
                           lambda kc: hT[:, kc, 0:256], lambda kc: [('hT', kc)], 256)
                    S.add('act', I('activation', out=kmT[:, head, :], in_=ps[bk][:, 0:256], func=AF.Copy),
                          reads=[('ps', bk)], writes=[('kmT', head)])
                w_release()
            vbk = [bank(), bank()]
            for ci in range(2):
                wt, wres = w_acquire('mv%d' % ci, li)
                wv3 = wt[:, 0:2048].rearrange("p (k m) -> p k m", k=8)
                for mt in range(2):
                    for kc in range(8):
                        S.add('pe', I('matmul',
                            ps[vbk[mt]][:, ci * 256:(ci + 1) * 256], lhsT=hT[:, kc, mt * 128:(mt + 1) * 128],
                            rhs=wv3[:, kc, :], start=(kc == 0), stop=(kc == 7)),
                            reads=[wres, ('hT', kc)], writes=[('ps', vbk[mt])])
                w_release()
            for mt in range(2):
                S.add('dve', I('tensor_copy', out=vmem[:, mt, :], in_=ps[vbk[mt]][:, :]),
                      reads=[('ps', vbk[mt])], writes=[('vmem', mt)])

            for si, (b0, nb, kvonly) in enumerate(sbs):
                N = nb * 128
                t0 = b0 * 128
                blocks = list(range(b0, b0 + nb))
                xr = xres(blocks)

                norm_in(li, 0, blocks, t0, N)
                hres = lambda kc: [('hT', kc)]
                hrhs = lambda kc: hT[:, kc, :N]
                qB, uB, vbB, yaB, ybB, ycB, plB = B[0], B[1], B[2], B[3], B[4], B[5], B[6]
                for ci in (() if kvonly else range(2)):
                    wt, wres = w_acquire('q%d' % ci, li)
                    wv3 = wt[:, 0:2048].rearrange("p (k m) -> p k m", k=8)
                    for hh in range(2):
                        c = ci * 2 + hh
                        bk = bank()
                        lin_fm(bk, lambda kc, hh=hh, wv3=wv3: wv3[:, kc, hh * 128:(hh + 1) * 128], wres, 8, hrhs, hres, N)
                        S.add('act', I('activation', out=qB[:, c, :N], in_=ps[bk][:, :N], func=AF.Copy),
                              reads=[('ps', bk)], writes=[('B0', c)])
                    w_release()
                wt, wres = w_acquire('k', li)
                wv3 = wt[:, 0:2048].rearrange("p (k m) -> p k m", k=8)
                for g in range(2):
                    bk = bank()
                    lin_fm(bk, lambda kc, g=g, wv3=wv3: wv3[:, kc, g * 128:(g + 1) * 128], wres, 8, hrhs, hres, N)
                    for i, b in enumerate(blocks):
                        s5 = b % 5
                        S.add('dve', I('tensor_copy',
                            out=kTe[0:64, g, s5 * 128:(s5 + 1) * 128], in_=ps[bk][0:64, i * 128:(i + 1) * 128]),
                            reads=[('ps', bk)], writes=[('kT', g, s5)])
                        S.add('dve', I('tensor_copy',
                            out=kTo[64:128, g, s5 * 128:(s5 + 1) * 128], in_=ps[bk][64:128, i * 128:(i + 1) * 128]),
                            reads=[('ps', bk)], writes=[('kT', g, s5)])
                w_release()
                wt, wres = w_acquire('v', li)
                wv3 = wt[:, 0:1024].rearrange("p (k m) -> p k m", k=8)
                bk = bank()
                for i, b in enumerate(blocks):
                    for kc in range(8):
                        S.add('pe', I('matmul',
                            ps[bk][:, i * 128:(i + 1) * 128], lhsT=hT[:, kc, i * 128:(i + 1) * 128], rhs=wv3[:, kc, :],
                            start=(kc == 0), stop=(kc == 7)),
                            reads=[wres, ('hT', kc)], writes=[('ps', bk)])
                for i, b in enumerate(blocks):
                    s5 = b % 5
                    S.add('dve', I('tensor_scalar',
                        out=v_sb[:, s5, :], in0=ps[bk][:, i * 128:(i + 1) * 128], scalar1=valid[:, b:b + 1], scalar2=None,
                        op0=ALU.mult),
                        reads=[('ps', bk), 'valid'], writes=[('v', s5)])
                w_release()
                for ci in (() if kvonly else range(2)):
                    wt, wres = w_acquire('su%d' % ci, li)
                    wv3 = wt[:, 0:2048].rearrange("p (k m) -> p k m", k=8)
                    for hh in range(2):
                        c = ci * 2 + hh
                        bk = bank()
                        lin_fm(bk, lambda kc, hh=hh, wv3=wv3: wv3[:, kc, hh * 128:(hh + 1) * 128], wres, 8, hrhs, hres, N)

                        def fin(t3, bk=bk, c=c):
                            S.add('dve', I('scalar_tensor_tensor', out=uB[:, c, :N], in0=T[t3][:, :N], scalar=1.0,
                                                                          in1=ps[bk][:, :N], op0=ALU.add, op1=ALU.mult),
                                  reads=[('ps', bk), ('T', t3)], writes=[('B1', c)])
                        gelu2(ps[bk][:, :N], [('ps', bk)], N, fin)
                    w_release()
                svb = [] if kvonly else [bank() for _ in range(nb)]
                for ci in (() if kvonly else range(2)):
                    wt, wres = w_acquire('sv%d' % ci, li)
                    wv3 = wt[:, 0:2048].rearrange("p (k m) -> p k m", k=8)
                    for i in range(nb):
                        for kc in range(8):
                            S.add('pe', I('matmul',
                                ps[svb[i]][:, ci * 256:(ci + 1) * 256], lhsT=hT[:, kc, i * 128:(i + 1) * 128],
                                rhs=wv3[:, kc, :], start=(kc == 0), stop=(kc == 7)),
                                reads=[wres, ('hT', kc)], writes=[('ps', svb[i])])
                    w_release()
                for i in (() if kvonly else range(nb)):
                    bk = svb[i]

                    def fin(t3, bk=bk, i=i):
                        tg = tmp()
                        S.add('dve', I('scalar_tensor_tensor', out=T[tg][:, :], in0=T[t3][:, :], scalar=1.0,
                                                                      in1=ps[bk][:, :], op0=ALU.add, op1=ALU.mult),
                              reads=[('ps', bk), ('T', t3)], writes=[('T', tg)])
                        tj = tmp()
                        S.add('dve', I('scalar_tensor_tensor', out=T[tj][:, :], in0=T[tg][:, :], scalar=1.0, in1=T[tg][:, :],
                                                                      op0=ALU.mult, op1=ALU.mult,
                                                                      accum_out=small[:, 0:1]),
                              reads=[('T', tg)], writes=[('T', tj), ('small', 0)])
                        S.add('dve', I('tensor_scalar', out=small[:, 1:2], in0=small[:, 0:1], scalar1=1.0 / 512.0,
                                                               scalar2=4.0 * EPS, op0=ALU.mult, op1=ALU.add),
                              reads=[('small', 0)], writes=[('small', 1)])
                        S.add('pool', I('tensor_tensor', out=small[:, 2:3], in0=small[:, 1:2], in1=negh[:, 0:1],
                                                                op=ALU.pow),
                              reads=[('small', 1), 'negh'], writes=[('small', 2)])
                        S.add('dve', I('scalar_tensor_tensor', out=vbB[:, i, :], in0=T[tg][:, :],
                                                                      scalar=small[:, 2:3], in1=bcs[:, 0:512],
                                                                      op0=ALU.mult, op1=ALU.mult),
                              reads=[('T', tg), ('small', 2), 'bc'], writes=[('B2', i)])
                    gelu2(ps[bk][:, :], [('ps', bk)], 512, fin)
                pcb = [bank() for _ in range(nb)]
                for ci in range(2):
                    wt, wres = w_acquire('pc%d' % ci, li)
                    wv3 = wt[:, 0:2048].rearrange("p (k m) -> p k m", k=8)
                    for i in range(nb):
                        for kc in range(8):
                            S.add('pe', I('matmul',
                                ps[pcb[i]][:, ci * 256:(ci + 1) * 256], lhsT=hT[:, kc, i * 128:(i + 1) * 128],
                                rhs=wv3[:, kc, :], start=(kc == 0), stop=(kc == 7)),
                                reads=[wres, ('hT', kc)], writes=[('ps', pcb[i])])
                    w_release()
                for i, b in enumerate(blocks):
                    s5 = b % 5
                    S.add('dve', I('tensor_scalar',
                        out=c_sb[:, s5, :], in0=ps[pcb[i]][:, :], scalar1=valid[:, b:b + 1], scalar2=None, op0=ALU.mult),
                        reads=[('ps', pcb[i]), 'valid'], writes=[('c', s5)])

                if kvonly:
                    continue

                items = [(i, g) for i in range(nb) for g in range(2)]

                def att_scores(it, n_it):
                    i, g = it
                    b = blocks[i]
                    kts = ([b - 1] if b > fb else []) + [b]
                    dbl = n_it % 2
                    sbk = []
                    for ki, kt in enumerate(kts):
                        own = (kt == b)
                        bk = bank()
                        sbk.append(bk)
                        s5 = kt % 5
                        for par in range(2):
                            S.add('pe', I('matmul',
                                ps[bk][:, par * 256:(par + 1) * 256],
                                lhsT=(kTe if par == 0 else kTo)[:, g, s5 * 128:(s5 + 1) * 128],
                                rhs=qB[:, 2 * g:2 * g + 2, i * 128:(i + 1) * 128],
                                start=True, stop=True),
                                reads=[('kT', g, s5), ('B0', 2 * g), ('B0', 2 * g + 1)], writes=[('ps', bk)])
                        pidx = dbl * 2 + (1 if own else 0)
                        pv = ptb[:, pidx, :].rearrange("p (a q) -> p a q", a=4)
                        sv_ = ps[bk][:, :].rearrange("p (a q) -> p a q", a=4)
                        if own:
                            full, part, qlo = slice(0, 64), slice(64, 128), 64
                        else:
                            full, part, qlo = slice(64, 128), slice(0, 64), 0
                        S.add('act', I('activation',
                            out=pv[full, :, :], in_=sv_[full, :, :], func=AF.Exp, scale=0.125),
                            reads=[('ps', bk)], writes=[('ptb', pidx)])
                        S.add('act', I('activation',
                            out=pv[part, :, qlo:qlo + 64], in_=sv_[part, :, qlo:qlo + 64], func=AF.Exp, scale=0.125),
                            reads=[('ps', bk)], writes=[('ptb', pidx)])
                    return (i, g, b, kts, dbl)

                def att_pv(stt):
                    i, g, b, kts, dbl = stt
                    ob = bank()
                    for which in range(2):
                        for par in range(2):
                            for ki, kt in enumerate(kts):
                                own = (kt == b)
                                pidx = dbl * 2 + (1 if own else 0)
                                s5 = kt % 5
                                if which == 0:
                                    lhs = v_sb[:, s5, g * 64:(g + 1) * 64]
                                    lres = ('v', s5)
                                else:
                                    lhs = vrep[:, kt, :]
                                    lres = ('vrep', kt)
                                S.add('pe', I('matmul',
                                    ps[ob][par * 64:(par + 1) * 64, which * 256:(which + 1) * 256], lhsT=lhs,
                                    rhs=ptb[:, pidx, par * 256:(par + 1) * 256], start=(ki == 0), stop=(ki == len(kts) - 1)),
                                    reads=[lres, ('ptb', pidx)], writes=[('ps', ob)])
                    ta = tmp()
                    r3 = lambda ap: ap.rearrange("p (a q) -> p a q", a=2)
                    S.add('dve', I('tensor_tensor', out=r3(T[ta][:, 0:256]), in0=r3(ps[ob][:, 256:512]),
                                                           in1=esb[:, 2 * g:2 * g + 2, :], op=ALU.add),
                          reads=[('ps', ob), 'esb'], writes=[('T', ta)])
                    tb = tmp()
                    S.add('dve', I('reciprocal', out=T[tb][:, 0:256], in_=T[ta][:, 0:256]),
                          reads=[('T', ta)], writes=[('T', tb)])
                    S.add('dve', I('tensor_tensor', out=yaB[:, 2 * g:2 * g + 2, i * 128:(i + 1) * 128],
                                                           in0=r3(ps[ob][:, 0:256]), in1=r3(T[tb][:, 0:256]), op=ALU.mult),
                          reads=[('ps', ob), ('T', tb)], writes=[('B3', 2 * g), ('B3', 2 * g + 1)])

                prev = None
                for n_it, it in enumerate(items):
                    cur = att_scores(it, n_it)
                    if prev is not None:
                        att_pv(prev)
                    prev = cur
                att_pv(prev)

                wt, wres = w_acquire('ws', li)
                ws3 = wt[:, 0:512].rearrange("p (g i) -> p g i", g=4)
                for i in range(nb):
                    bk = bank()
                    for g in range(4):
                        S.add('pe', I('matmul',
                            ps[bk][:, g * 128:g * 128 + 64], lhsT=vbB[0:64, i, g * 128:(g + 1) * 128], rhs=ws3[0:64, g, 0:64],
                            start=True, stop=True),
                            reads=[wres, ('B2', i)], writes=[('ps', bk)])
                        S.add('pe', I('matmul',
                            ps[bk][:, g * 128 + 64:(g + 1) * 128], lhsT=vbB[:, i, g * 128:(g + 1) * 128], rhs=ws3[:, g, 64:128],
                            start=True, stop=True),
                            reads=[wres, ('B2', i)], writes=[('ps', bk)])
                    ta = tmp()
                    S.add('dve', I('tensor_tensor', out=T[ta][:, :], in0=ps[bk][:, :], in1=bcs[:, 512:1024],
                                                                        op=ALU.add),
                          reads=[('ps', bk), 'bc'], writes=[('T', ta)])
                    S.add('dve', I('scalar_tensor_tensor',
                        out=ybB[:, :, i * 128:(i + 1) * 128], in0=uB[:, :, i * 128:(i + 1) * 128], scalar=0.5,
                        in1=T[ta][:, :].rearrange("p (g q) -> p g q", g=4), op0=ALU.mult, op1=ALU.mult),
                        reads=[('T', ta)] + [('B1', c) for c in range(4)], writes=[('B4', c) for c in range(4)])
                w_release()

                for i, b in enumerate(blocks):
                    bk = bank()
                    var = 1 if b == 4 else 0
                    hasprev = b > fb
                    for g in range(4):
                        S.add('pe', I('matmul',
                            ps[bk][:, g * 128:(g + 1) * 128], lhsT=c_sb[:, b % 5, g * 128:(g + 1) * 128],
                            rhs=ptm_v[:, var, g, 0, :], start=True, stop=(not hasprev)),
                            reads=[('c', b % 5), 'ptm'], writes=[('ps', bk)])
                        if hasprev:
                            S.add('pe', I('matmul',
                                ps[bk][:, g * 128:(g + 1) * 128], lhsT=c_sb[:, (b - 1) % 5, g * 128:(g + 1) * 128],
                                rhs=ptm_v[:, var, g, 1, :], start=False, stop=True),
                                reads=[('c', (b - 1) % 5), 'ptm'], writes=[('ps', bk)])
                    S.add('act', I('activation',
                        out=plB[:, :, i * 128:(i + 1) * 128], in_=ps[bk][:, :].rearrange("p (g q) -> p g q", g=4), func=AF.Copy),
                        reads=[('ps', bk)], writes=[('B6', c) for c in range(4)])
                wt, wres = w_acquire('wp', li)
                wp3 = wt[:, 0:512].rearrange("p (g d) -> p g d", g=4)
                for g in range(4):
                    bk = bank()
                    S.add('pe', I('matmul', ps[bk][:, :N], lhsT=wp3[:, g, :], rhs=plB[:, g, :N],
                                                               start=True, stop=True),
                          reads=[wres, ('B6', g)], writes=[('ps', bk)])
                    S.add('dve', I('tensor_scalar', out=ycB[:, g, :N], in0=ps[bk][:, :N],
                                                                      scalar1=vecs[:, vb0 + 48 + g:vb0 + 49 + g], scalar2=None,
                                                                      op0=ALU.mult),
                          reads=[('ps', bk), 'vecs'], writes=[('B5', g)])
                w_release()

                brs = [(yaB, 'B3'), (ybB, 'B4'), (ycB, 'B5')]
                for dc in range(8):
                    mB, mname, mc = (B[0], 'B0', dc) if dc < 4 else (B[1], 'B1', dc - 4)
                    acc = None
                    for n in range(3):
                        wt, wres = w_acquire('gb%d_%d' % (dc, n), li)
                        gv = wt[:, 0:1024].rearrange("p (k m) -> p k m", k=8)
                        bv = wt[:, 1024:1536].rearrange("p (k m) -> p k m", k=4)
                        gk = bank()
                        lin_fm(gk, lambda kc, gv=gv: gv[:, kc, :], wres, 8, hrhs, hres, N)
                        tth = tmp()
                        S.add('act', I('activation', out=T[tth][:, :N], in_=ps[gk][:, :N], func=AF.Tanh,
                                                                           scale=0.5),
                              reads=[('ps', gk)], writes=[('T', tth)])
                        pk = bank()
                        brB, brn = brs[n]
                        lin_fm(pk, lambda kc, bv=bv: bv[:, kc, :], wres, 4, lambda kc, brB=brB: brB[:, kc, :N],
                               lambda kc, brn=brn: [(brn, kc)], N)
                        w_release()
                        tp = tmp()
                        S.add('dve', I('scalar_tensor_tensor',
                            out=T[tp][:, :N], in0=T[tth][:, :N], scalar=1.0, in1=ps[pk][:, :N], op0=ALU.add, op1=ALU.mult),
                            reads=[('T', tth), ('ps', pk)], writes=[('T', tp)])
                        if n == 0:
                            acc = tp
                        elif n == 1:
                            ta2 = tmp()
                            S.add('pool', I('tensor_tensor',
                                out=T[ta2][:, :N], in0=T[acc][:, :N], in1=T[tp][:, :N], op=ALU.add),
                                reads=[('T', acc), ('T', tp)], writes=[('T', ta2)])
                            acc = ta2
                        else:
                            S.add('pool', I('tensor_tensor',
                                out=mB[:, mc, :N], in0=T[acc][:, :N], in1=T[tp][:, :N], op=ALU.add),
                                reads=[('T', acc), ('T', tp)], writes=[(mname, mc)])

                ep = Epilogue(li, 1, blocks, t0, N, 0.5)
                mrhs = lambda kc: (B[0][:, kc, :N] if kc < 4 else B[1][:, kc - 4, :N])
                mres = lambda kc: [('B0', kc)] if kc < 4 else [('B1', kc - 4)]
                for ci in range(4):
                    wt, wres = w_acquire('wo%d' % ci, li)
                    wv3 = wt[:, 0:2048].rearrange("p (k m) -> p k m", k=8)
                    for hh in range(2):
                        dc = ci * 2 + hh
                        bk = bank()
                        lin_fm(bk, lambda kc, hh=hh, wv3=wv3: wv3[:, kc, hh * 128:(hh + 1) * 128], wres, 8, mrhs, mres, N)
                        ep.push(dc, bk)
                    w_release()
                ep.finish()

                norm_in(li, 2, blocks, t0, N)
                qmB, pmB, omB = B[2], B[6], B[7]
                for ci in range(2):
                    wt, wres = w_acquire('wq%d' % ci, li)
                    wv3 = wt[:, 0:2048].rearrange("p (k m) -> p k m", k=8)
                    for hh in range(2):
                        head = ci * 2 + hh
                        bk = bank()
                        lin_fm(bk, lambda kc, hh=hh, wv3=wv3: wv3[:, kc, hh * 128:(hh + 1) * 128], wres, 8, hrhs, hres, N)
                        S.add('act', I('activation', out=qmB[:, head, :N], in_=ps[bk][:, :N], func=AF.Copy),
                              reads=[('ps', bk)], writes=[('B2', head)])
                    w_release()
                mscale = 1.0 / math.sqrt(128.0)

                def mem_scores(head):
                    for mt in range(2):
                        bk = bank()
                        pi_ = (head % 2) * 2 + mt
                        S.add('pe', I('matmul', ps[bk][:, :N], lhsT=kmT[:, head, mt * 128:(mt + 1) * 128],
                                                                     rhs=qmB[:, head, :N], start=True, stop=True),
                              reads=[('kmT', head), ('B2', head)], writes=[('ps', bk)])
                        S.add('act', I('activation', out=pmB[:, pi_, :N], in_=ps[bk][:, :N], func=AF.Exp,
                                                                           scale=mscale),
                              reads=[('ps', bk)], writes=[('B6', pi_)])

                def mem_pv(head):
                    ob, db = bank(), bank()
                    for mt in range(2):
                        pi_ = (head % 2) * 2 + mt
                        S.add('pe', I('matmul', ps[ob][:, :N], lhsT=vmem[:, mt, head * 128:(head + 1) * 128],
                                                                      rhs=pmB[:, pi_, :N], start=(mt == 0), stop=(mt == 1)),
                              reads=[('vmem', mt), ('B6', pi_)], writes=[('ps', ob)])
                    for mt in range(2):
                        pi_ = (head % 2) * 2 + mt
                        S.add('pe', I('matmul', ps[db][:, :N], lhsT=ones_1[:, :], rhs=pmB[:, pi_, :N],
                                                                      start=(mt == 0), stop=(mt == 1)),
                              reads=['ones_1', ('B6', pi_)], writes=[('ps', db)])
                    ta = tmp()
                    S.add('dve', I('reciprocal', out=T[ta][:, :N], in_=ps[db][:, :N]),
                          reads=[('ps', db)], writes=[('T', ta)])
                    S.add('dve', I('tensor_tensor', out=omB[:, head, :N], in0=ps[ob][:, :N], in1=T[ta][:, :N], op=ALU.mult),
                          reads=[('ps', ob), ('T', ta)], writes=[('B7', head)])

                mem_scores(0)
                for head in range(4):
                    if head + 1 < 4:
                        mem_scores(head + 1)
                    mem_pv(head)
                ep = Epilogue(li, 3, blocks, t0, N, 1.0)
                for ci in range(2):
                    wt, wres = w_acquire('wom%d' % ci, li)
                    wv3 = wt[:, 0:2048].rearrange("p (k m) -> p k m", k=4)
                    for hh in range(4):
                        dc = ci * 4 + hh
                        bk = bank()
                        lin_fm(bk, lambda kc, hh=hh, wv3=wv3: wv3[:, kc, hh * 128:(hh + 1) * 128], wres, 4,
                               lambda kc: omB[:, kc, :N], lambda kc: [('B7', kc)], N)
                        ep.push(dc, bk)
                    w_release()
                ep.finish()

                norm_in(li, 4, blocks, t0, N)
                for ci in range(16):
                    wt, wres = w_acquire('up%d' % ci, li)
                    wv3 = wt[:, 0:2048].rearrange("p (k m) -> p k m", k=8)
                    for hh in range(2):
                        j = ci * 2 + hh
                        bk = bank()
                        lin_fm(bk, lambda kc, hh=hh, wv3=wv3: wv3[:, kc, hh * 128:(hh + 1) * 128], wres, 8, hrhs, hres, N)
                        tr = tmp()
                        S.add('act', I('activation', out=T[tr][:, :N], in_=ps[bk][:, :N], func=AF.Relu),
                              reads=[('ps', bk)], writes=[('T', tr)])
                        S.add('dve', I('tensor_tensor', out=B[j // 4][:, j % 4, :N], in0=T[tr][:, :N],
                                                                                 in1=ps[bk][:, :N], op=ALU.mult),
                              reads=[('ps', bk), ('T', tr)], writes=[('B%d' % (j // 4), j % 4)])
                    w_release()
                ep = Epilogue(li, 5, blocks, t0, N, 1.0)
                for dc in range(8):
                    bk = bank()
                    for half in range(2):
                        wt, wres = w_acquire('dn%d%s' % (dc, 'ab'[half]), li)
                        wv3 = wt[:, 0:2048].rearrange("p (k m) -> p k m", k=16)
                        for kk in range(16):
                            j = half * 16 + kk
                            S.add('pe', I('matmul',
                                ps[bk][:, :N], lhsT=wv3[:, kk, :], rhs=B[j // 4][:, j % 4, :N],
                                start=(j == 0), stop=(j == 31)),
                                reads=[wres, ('B%d' % (j // 4), j % 4)], writes=[('ps', bk)])
                        w_release()
                    ep.push(dc, bk)
                ep.finish()

                if li == nL - 1 and b0 >= 4:
                    o0 = t0 - HALO
                    S.add('sp', I('dma_start', out=out_d[:, :, o0:o0 + N], in_=xT[:, :, t0:t0 + N]),
                          reads=xr, writes=[('out', b0)], dma='st')

        S.add('sp', I('nop', ), reads=[('out', b) for b in (4, 8, 12, 16)][:(None if max_sbs is None else max(max_sbs - 1, 0))])
        assert wstate['acq'] == len(seq), (wstate, len(seq))
        S.emit(nc)
    return nc


def superblocks(fb):
    sbs = [(fb, 1, True)]
    if fb + 1 < 4:
        sbs.append((fb + 1, 4 - (fb + 1), False))
    for k in range(1, 5):
        sbs.append((4 * k, 4, False))
    return sbs


KV_CHUNKS = ('k', 'v', 'pc0', 'pc1')


def make_ptm(first_variant_is_seq_start):
    P = np.zeros((128, 2, 4, 2, 128), np.float32)
    s = np.arange(128)[:, None]
    t = np.arange(128)[None, :]
    for g, w in enumerate((2, 4, 8, 16)):
        own = ((s <= t) & (s > t - w)).astype(np.float32) / w - (s == t).astype(np.float32)
        prv = ((s + 0) > (128 + t - w)).astype(np.float32) / w
        P[:, 0, g, 0, :] = own
        P[:, 0, g, 1, :] = prv
        if first_variant_is_seq_start:
            cnt = np.minimum(t + 1, w).astype(np.float32)
            P[:, 1, g, 0, :] = ((s <= t) & (s > t - w)).astype(np.float32) / cnt - (s == t).astype(np.float32)
            P[:, 1, g, 1, :] = 0.0
        else:
            P[:, 1, g, 0, :] = own
            P[:, 1, g, 1, :] = prv
    return P.reshape(128, -1)


def make_vecs(layers, g_norm, g_mem, pool_scale, attn_sinks):
    nL = len(layers)
    v = np.zeros((128, nL * NV), np.float32)
    for i, l in enumerate(layers):
        base = i * NV
        for gi in range(6):
            v[:, base + gi * 8: base + gi * 8 + 8] = g_norm[l, gi].reshape(8, 128).T
        v[:, base + 48: base + 52] = pool_scale[l].reshape(4, 128).T
        v[:, base + 52: base + 60] = g_mem[l].reshape(8, 128).T
        for c in range(4):
            v[0:64, base + 60 + c] = attn_sinks[l, 2 * c]
            v[64:128, base + 60 + c] = attn_sinks[l, 2 * c + 1]
    return v


def make_bc(layers, g_sgu, b_spatial):
    nL = len(layers)
    bc = np.zeros((nL, 128, NBC), np.float32)
    for i, l in enumerate(layers):
        bc[i, :, 0:512] = g_sgu[l][None, :]
        bc[i, :, 512:1024] = b_spatial[l].reshape(1, 512)
    return bc


_PROG_CACHE = {}


def _get_prog(nL, first_blocks):
    key = (nL, tuple(first_blocks))
    if key not in _PROG_CACHE:
        _PROG_CACHE[key] = build_program(list(range(nL)), list(first_blocks))
    return _PROG_CACHE[key]


FUSED = True


def kernel(x, mem, g_norm, g_mem, w_in, attn_sinks, w_spatial, b_spatial, g_sgu, w_pool, pool_scale,
           w_branch, w_out, w_q_mem, w_kv_mem, w_o_mem, w_up, w_down):
    f = lambda a: np.ascontiguousarray(np.asarray(a, dtype=np.float32))
    x, mem, g_norm, g_mem, w_in, attn_sinks = f(x), f(mem), f(g_norm), f(g_mem), f(w_in), f(attn_sinks)
    w_spatial, b_spatial, g_sgu, w_pool, pool_scale = f(w_spatial), f(b_spatial), f(g_sgu), f(w_pool), f(pool_scale)
    w_branch, w_out, w_q_mem, w_kv_mem, w_o_mem, w_up, w_down = (f(w_branch), f(w_out), f(w_q_mem), f(w_kv_mem),
                                                                 f(w_o_mem), f(w_up), f(w_down))
    wl = [pack_layer_weights(l, w_in, w_spatial, w_pool, w_branch, w_out, w_q_mem, w_kv_mem, w_o_mem, w_up, w_down)
          for l in range(DEPTH)]

    def core_consts(c):
        half = c % 2
        valid = np.ones((128, NBLK), np.float32)
        if half == 0:
            valid[:, 0:4] = 0.0
        return valid, make_ptm(half == 0)

    def x_layout(xcur):
        outs = []
        for c in range(NCORES):
            b, half = c // 2, c % 2
            start = half * TOK - HALO
            xs = np.zeros((TT, D), np.float32)
            lo = max(start, 0)
            xs[lo - start:, :] = xcur[b, lo:start + TT, :]
            outs.append(np.ascontiguousarray(xs.reshape(TT, 8, 128).transpose(2, 1, 0)))
        return outs

    memT = [np.ascontiguousarray(mem[c // 2].reshape(256, 8, 128).transpose(2, 1, 0)) for c in range(NCORES)]
    consts = [core_consts(c) for c in range(NCORES)]

    def run(layers, first_blocks, xcur):
        nc = _get_prog(len(layers), first_blocks)
        wst = np.stack([wl[l] for l in layers], axis=0)
        vecs = make_vecs(layers, g_norm, g_mem, pool_scale, attn_sinks)
        bc = make_bc(layers, g_sgu, b_spatial)
        xs = x_layout(xcur)
        in_maps = []
        for c in range(NCORES):
            in_maps.append({"xT_in": xs[c], "memT": memT[c], "wst": wst, "vecs": vecs, "bc": bc,
                            "valid": consts[c][0], "ptm": consts[c][1]})
        res = run_bass_kernel_spmd(nc, in_maps, core_ids=list(range(NCORES)))
        out = np.zeros((BATCH, SEQ, D), np.float32)
        for c in range(NCORES):
            b, half = c // 2, c % 2
            o = np.asarray(res.results[c]["outT"])
            out[b, half * TOK:(half + 1) * TOK, :] = o.transpose(2, 1, 0).reshape(TOK, D)
        return out

    if FUSED:
        return run(list(range(DEPTH)), [0, 1, 2, 3], x)
    xcur = x
    for l in range(DEPTH):
        xcur = run([l], [3], xcur)
    return xcur
```

```python
import contextlib
import math
import numpy as np
import concourse.bass as bass
import concourse.mybir as mybir
from concourse.alu_op_type import AluOpType as ALU
from concourse.bass_utils import run_bass_kernel_spmd

F32 = mybir.dt.float32
BF16 = mybir.dt.bfloat16
AF = mybir.ActivationFunctionType

D = 1024
SEQ = 4096
BATCH = 4
DEPTH = 4
NCORES = 8
TOK = 2048
HALO = 512
TT = TOK + HALO
NBLK = TT // 128
IN_DIM = 5376
EPS = 1e-6
SLOT = 2048
RING = 5
NV = 64
NBC = 1024

COMPUTE = ('pe', 'act', 'dve', 'pool')
QUEUES = ('pe', 'act', 'dve', 'pool', 'sp')


def I(method, *args, **kwargs):
    return lambda h: getattr(h, method)(*args, **kwargs)


class Op:
    __slots__ = ('eng', 'fn', 'deps', 'dma', 'waits', 'sig', 'sigval', 'dsem', 'dval', 'idx')


class Sched:
    def __init__(self):
        self.ops = []
        self.last_w = {}
        self.readers = {}
        self.dma_sems = {}

    def add(self, eng, fn, reads=(), writes=(), dma=None):
        op = Op()
        op.eng, op.fn, op.dma = eng, fn, dma
        op.idx = len(self.ops)
        op.sig = False
        op.sigval = None
        op.waits = []
        if dma is not None:
            n = self.dma_sems.get(dma, 0) + 1
            self.dma_sems[dma] = n
            op.dsem, op.dval = dma, 16 * n
        deps = {}
        for r in reads:
            w = self.last_w.get(r)
            if w is not None:
                deps[w.idx] = w
            if isinstance(r, tuple) and r[0] == 'ps':
                for o in self.readers.get(r, ()):
                    if o.eng != eng:
                        deps[o.idx] = o
        for wkey in writes:
            w = self.last_w.get(wkey)
            if w is not None:
                deps[w.idx] = w
            for o in self.readers.get(wkey, ()):
                deps[o.idx] = o
        fdeps = []
        for d in deps.values():
            if d is op:
                continue
            if d.dma is None and dma is None and d.eng == eng and eng == 'pe':
                continue
            fdeps.append(d)
        op.deps = fdeps
        for wkey in writes:
            self.last_w[wkey] = op
            self.readers[wkey] = []
        for r in reads:
            lst = self.readers.setdefault(r, [])
            if dma is None:
                lst[:] = [o for o in lst if not (o.dma is None and o.eng == eng)]
            lst.append(op)
        self.ops.append(op)
        return op

    def finalize(self):
        for op in self.ops:
            for d in op.deps:
                if d.dma is None:
                    d.sig = True
        cnt = {e: 0 for e in QUEUES}
        for op in self.ops:
            if op.dma is None and op.sig:
                cnt[op.eng] += 1
                op.sigval = cnt[op.eng]
        waited = {e: {} for e in QUEUES}
        for op in self.ops:
            need = {}
            for d in op.deps:
                if d.dma is not None:
                    key, val = 'dma:' + d.dsem, d.dval
                else:
                    key, val = d.eng, d.sigval
                if val > need.get(key, 0):
                    need[key] = val
            wq = waited[op.eng]
            for key, val in need.items():
                if wq.get(key, 0) >= val:
                    continue
                wq[key] = val
                op.waits.append((key, val))

    def emit(self, nc):
        self.finalize()
        with contextlib.ExitStack() as st:
            sems = {}
            for e in QUEUES:
                sems[e] = st.enter_context(nc.semaphore('s_' + e))
            for name in self.dma_sems:
                sems['dma:' + name] = st.enter_context(nc.semaphore('d_' + name))
            block = st.enter_context(nc.Block())
            per = {e: [op for op in self.ops if op.eng == e] for e in QUEUES}

            def run(handle, ename):
                for op in per[ename]:
                    for key, val in op.waits:
                        handle.wait_ge(sems[key], val)
                    ins = op.fn(handle)
                    if op.dma is not None:
                        ins.then_inc(sems['dma:' + op.dsem], 16)
                    elif op.sig:
                        ins.then_inc(sems[ename], 1)

            @block.tensor
            def _(h):
                run(h, 'pe')

            @block.scalar
            def _(h):
                run(h, 'act')

            @block.vector
            def _(h):
                run(h, 'dve')

            @block.gpsimd
            def _(h):
                run(h, 'pool')

            @block.sync
            def _(h):
                run(h, 'sp')


def layer_chunks():
    ch = [('q0', 2048), ('q1', 2048), ('k', 2048), ('v', 1024),
          ('su0', 2048), ('su1', 2048), ('sv0', 2048), ('sv1', 2048), ('pc0', 2048), ('pc1', 2048),
          ('ws', 512), ('wp', 512)]
    for dc in range(8):
        for n in range(3):
            ch.append(('gb%d_%d' % (dc, n), 1536))
    for i in range(4):
        ch.append(('wo%d' % i, 2048))
    ch += [('wq0', 2048), ('wq1', 2048), ('wom0', 2048), ('wom1', 2048)]
    for i in range(16):
        ch.append(('up%d' % i, 2048))
    for dc in range(8):
        ch.append(('dn%da' % dc, 2048))
        ch.append(('dn%db' % dc, 2048))
    return ch


def mem_chunks():
    return [('mk0', 2048), ('mk1', 2048), ('mv0', 2048), ('mv1', 2048)]


def chunk_offsets():
    offs = {}
    o = 0
    for name, size in mem_chunks() + layer_chunks():
        offs[name] = (o, size)
        o += size
    return offs, o


def _kblk(W, cols):
    K = W.shape[0]
    Wc = W[:, cols]
    return np.ascontiguousarray(Wc.reshape(K // 128, 128, Wc.shape[1]).transpose(1, 0, 2)).reshape(128, -1)


def pack_layer_weights(l, w_in, w_spatial, w_pool, w_branch, w_out, w_q_mem, w_kv_mem, w_o_mem, w_up, w_down):
    offs, tot = chunk_offsets()
    out = np.zeros((128, tot), np.float32)

    def put(name, arr):
        o, size = offs[name]
        assert arr.shape == (128, size), (name, arr.shape, size)
        out[:, o:o + size] = arr

    Wi = w_in[l]
    ar = np.arange
    put('q0', _kblk(Wi, ar(0, 256)))
    put('q1', _kblk(Wi, ar(256, 512)))
    kcols = np.concatenate([ar(512, 576), ar(512, 576), ar(576, 640), ar(576, 640)])
    put('k', _kblk(Wi, kcols))
    put('v', _kblk(Wi, ar(640, 768)))
    put('su0', _kblk(Wi, ar(768, 1024)))
    put('su1', _kblk(Wi, ar(1024, 1280)))
    put('sv0', _kblk(Wi, ar(1280, 1536)))
    put('sv1', _kblk(Wi, ar(1536, 1792)))
    put('pc0', _kblk(Wi, ar(1792, 2048)))
    put('pc1', _kblk(Wi, ar(2048, 2304)))
    put('ws', np.ascontiguousarray(w_spatial[l].transpose(2, 0, 1)).reshape(128, 512))
    put('wp', np.ascontiguousarray(w_pool[l].transpose(1, 0, 2)).reshape(128, 512))
    for dc in range(8):
        for n in range(3):
            g = _kblk(Wi, ar(2304 + n * 1024 + dc * 128, 2304 + n * 1024 + (dc + 1) * 128))
            b = _kblk(w_branch[l, n], ar(dc * 128, (dc + 1) * 128))
            put('gb%d_%d' % (dc, n), np.concatenate([g, b], axis=1))
    for i in range(4):
        put('wo%d' % i, _kblk(w_out[l], ar(i * 256, (i + 1) * 256)))
    put('wq0', _kblk(w_q_mem[l], ar(0, 256)))
    put('wq1', _kblk(w_q_mem[l], ar(256, 512)))
    put('wom0', _kblk(w_o_mem[l], ar(0, 512)))
    put('wom1', _kblk(w_o_mem[l], ar(512, 1024)))
    for i in range(16):
        put('up%d' % i, _kblk(w_up[l], ar(i * 256, (i + 1) * 256)))
    for dc in range(8):
        full = _kblk(w_down[l], ar(dc * 128, (dc + 1) * 128)).reshape(128, 32, 128)
        put('dn%da' % dc, full[:, 0:16, :].reshape(128, 2048))
        put('dn%db' % dc, full[:, 16:32, :].reshape(128, 2048))
    Wkv = w_kv_mem[l]
    put('mk0', _kblk(Wkv, ar(0, 256)))
    put('mk1', _kblk(Wkv, ar(256, 512)))
    put('mv0', _kblk(Wkv, ar(512, 768)))
    put('mv1', _kblk(Wkv, ar(768, 1024)))
    return out


def build_program(layers, first_blocks, max_sbs=None):
    nL = len(layers)
    offs, WTOT = chunk_offsets()
    nc = bass.Bass("TRN2", target_bir_lowering=False)
    x_in = nc.dram_tensor("xT_in", [128, 8, TT], F32, kind="ExternalInput").ap()
    mem_in = nc.dram_tensor("memT", [128, 8, 256], F32, kind="ExternalInput").ap()
    wst = nc.dram_tensor("wst", [nL, 128, WTOT], F32, kind="ExternalInput").ap()
    vecs_in = nc.dram_tensor("vecs", [128, nL * NV], F32, kind="ExternalInput").ap()
    bc_in = nc.dram_tensor("bc", [nL, 128, NBC], F32, kind="ExternalInput").ap()
    valid_in = nc.dram_tensor("valid", [128, NBLK], F32, kind="ExternalInput").ap()
    ptm_in = nc.dram_tensor("ptm", [128, 2 * 4 * 2 * 128], F32, kind="ExternalInput").ap()
    out_d = nc.dram_tensor("outT", [128, 8, TOK], F32, kind="ExternalOutput").ap()

    S = Sched()
    st = contextlib.ExitStack()
    with st:
        def sb(name, shape, dt):
            return st.enter_context(nc.sbuf_tensor(name, shape, dt))

        xT = sb("xT", [128, 8, TT], F32)
        hT = sb("hT", [128, 8, 512], BF16)
        B = [sb("B%d" % i, [128, 4, 512], BF16) for i in range(8)]
        ybuf = sb("ybuf", [128, 8, 512], F32)
        T = [sb("T%d" % i, [128, 512], F32) for i in range(5)]
        rstdT = sb("rstdT", [128, 512], F32)
        sq = sb("sq", [128, 2, 512], BF16)
        ptb = sb("ptb", [128, 4, 512], BF16)
        kTe = sb("kTe", [128, 2, 5 * 128], BF16)
        kTo = sb("kTo", [128, 2, 5 * 128], BF16)
        v_sb = sb("v_sb", [128, 5, 128], BF16)
        vrep = sb("vrep", [128, NBLK, 64], BF16)
        c_sb = sb("c_sb", [128, 5, 512], BF16)
        kmT = sb("kmT", [128, 4, 256], BF16)
        vmem = sb("vmem", [128, 2, 512], BF16)
        ring = [sb("ring%d" % i, [128, SLOT], BF16) for i in range(RING)]
        vecs = sb("vecs_sb", [128, nL * NV], F32)
        bcs = sb("bc_sb", [128, NBC], F32)
        valid = sb("valid_sb", [128, NBLK], F32)
        ptm = sb("ptm_sb", [128, 2 * 4 * 2 * 128], BF16)
        ones_m = sb("ones_m", [128, 128], BF16)
        ones_1 = sb("ones_1", [128, 128], BF16)
        onesf = sb("onesf", [128, 128], F32)
        negh = sb("negh", [128, 512], F32)
        esb = sb("esb", [128, 4, 128], F32)
        small = sb("small", [128, 8], F32)
        ps = [st.enter_context(nc.psum_tensor("ps%d" % i, [128, 512], F32)) for i in range(8)]

        ptm_v = ptm[:, :].rearrange("p (v g o t) -> p v g o t", v=2, g=4, o=2)

        seq = []
        for li in range(nL):
            fb = first_blocks[li]
            sbs = [sb_ for sb_ in superblocks(fb) if max_sbs is None or sb_[0] < 4 * max_sbs]
            for si in range(len(sbs)):
                if si == 0:
                    seq += [(li, n, s) for n, s in mem_chunks()]
                if sbs[si][2]:
                    seq += [(li, n, s) for n, s in layer_chunks() if n in KV_CHUNKS]
                else:
                    seq += [(li, n, s) for n, s in layer_chunks()]
        wstate = {'issued': 0, 'acq': 0}

        def w_issue():
            i = wstate['issued']
            if i >= len(seq):
                return
            li, name, size = seq[i]
            o, _ = offs[name]
            slot = i % RING
            S.add('pool', I('dma_start',
                out=ring[slot][:, 0:size], in_=wst[li, :, o:o + size]),
                writes=[('w', slot)], dma='w%d' % slot)
            wstate['issued'] = i + 1

        def w_acquire(name, li):
            i = wstate['acq']
            assert i < wstate['issued'], "weight ring deadlock: acquire before issue (%s)" % name
            assert seq[i][0] == li and seq[i][1] == name, (seq[i], li, name)
            wstate['acq'] = i + 1
            slot = i % RING
            return ring[slot], ('w', slot)

        def w_release():
            w_issue()

        rot = {'g': 0, 's': 0}

        def bank():
            b = rot['g'] % 6
            rot['g'] += 1
            return b

        def sbank():
            b = 6 + rot['s'] % 2
            rot['s'] += 1
            return b

        trot = {'i': 0}

        def tmp():
            i = trot['i'] % 5
            trot['i'] += 1
            return i

        S.add('sp', I('dma_start', out=vecs[:, :], in_=vecs_in), writes=['vecs'], dma='c0')
        S.add('sp', I('dma_start', out=valid[:, :], in_=valid_in), writes=['valid'], dma='c1')
        S.add('pool', I('dma_start', out=ptm[:, :], in_=ptm_in), writes=['ptm'], dma='c2')
        nsb_x = TT // 512 if max_sbs is None else min(TT // 512, max_sbs)
        for i in range(nsb_x):
            S.add('sp', I('dma_start', out=xT[:, :, i * 512:(i + 1) * 512],
                                                   in_=x_in[:, :, i * 512:(i + 1) * 512]),
                  writes=[('x', b, d) for b in range(4 * i, 4 * i + 4) for d in range(8)], dma='x%d' % i)
        for _ in range(RING):
            w_issue()
        S.add('dve', I('memset', ones_m[:, :], 1.0 / 1024.0), writes=['ones_m'])
        S.add('dve', I('memset', ones_1[:, :], 1.0), writes=['ones_1'])
        S.add('dve', I('memset', onesf[:, :], 1.0), writes=['onesf'])
        S.add('dve', I('memset', negh[:, :], -0.5), writes=['negh'])
        S.add('dve', I('memset', ptb[:, :, :], 0.0), writes=[('ptb', i) for i in range(4)])
        S.add('dve', I('memset', kTe[:, :, :], 0.0), writes=[('kT', g, s5) for g in range(2) for s5 in range(5)])
        S.add('dve', I('memset', kTo[:, :, :], 0.0), writes=[('kT', g, s5) for g in range(2) for s5 in range(5)])
        for b in range(NBLK):
            S.add('dve', I('tensor_scalar', out=vrep[:, b, :], in0=onesf[:, 0:64], scalar1=valid[:, b:b + 1],
                                                       scalar2=None, op0=ALU.mult),
                  reads=['onesf', 'valid'], writes=[('vrep', b)])

        def xres(blocks, dc=None):
            if dc is None:
                return [('x', b, d) for b in blocks for d in range(8)]
            return [('x', b, dc) for b in blocks]

        def rstd_from(ssb, N):
            ta = tmp()
            S.add('dve', I('tensor_scalar', out=T[ta][:, :N], in0=ps[ssb][:, :N], scalar1=EPS, scalar2=None,
                                                   op0=ALU.add),
                  reads=[('ps', ssb)], writes=[('T', ta)])
            tb = tmp()
            S.add('act', I('activation', out=T[tb][:, :N], in_=T[ta][:, :N], func=AF.Ln),
                  reads=[('T', ta)], writes=[('T', tb)])
            S.add('act', I('activation', out=rstdT[:, :N], in_=T[tb][:, :N], func=AF.Exp, scale=-0.5),
                  reads=[('T', tb)], writes=['rstd'])
            return None

        def norm_in(li, gi, blocks, t0, N):
            ssb = sbank()
            for dc in range(8):
                S.add('act', I('activation', out=sq[:, dc % 2, :N], in_=xT[:, dc, t0:t0 + N], func=AF.Square),
                      reads=xres(blocks, dc), writes=[('sq', dc % 2)])
                S.add('pe', I('matmul', ps[ssb][:, :N], lhsT=ones_m[:, :], rhs=sq[:, dc % 2, :N],
                                                      start=(dc == 0), stop=(dc == 7)),
                      reads=[('sq', dc % 2), 'ones_m'], writes=[('ps', ssb)])
            rb = rstd_from(ssb, N)
            gcol = li * NV + gi * 8
            for dc in range(8):
                S.add('dve', I('scalar_tensor_tensor',
                    out=hT[:, dc, :N], in0=xT[:, dc, t0:t0 + N], scalar=vecs[:, gcol + dc:gcol + dc + 1],
                    in1=rstdT[:, :N], op0=ALU.mult, op1=ALU.mult),
                    reads=xres(blocks, dc) + ['rstd', 'vecs'], writes=[('hT', dc)])

        class Epilogue:
            def __init__(self, li, gi, blocks, t0, N, yscale):
                self.li, self.gi, self.blocks, self.t0, self.N, self.ys = li, gi, blocks, t0, N, yscale
                self.ssb = sbank()
                self.pending = None
                self.gcol = li * NV + gi * 8

            def push(self, dc, bk):
                N, ys = self.N, self.ys
                gc = self.gcol + dc
                S.add('act', I('activation', out=sq[:, dc % 2, :N], in_=ps[bk][:, :N], func=AF.Square, scale=ys),
                      reads=[('ps', bk)], writes=[('sq', dc % 2)])
                S.add('dve', I('tensor_scalar', out=ybuf[:, dc, :N], in0=ps[bk][:, :N], scalar1=vecs[:, gc:gc + 1],
                                                       scalar2=ys, op0=ALU.mult, op1=ALU.mult),
                      reads=[('ps', bk), 'vecs'], writes=[('ybuf', dc)])
                self.flush()
                self.pending = dc

            def flush(self):
                if self.pending is None:
                    return
                dc, N, ssb = self.pending, self.N, self.ssb
                S.add('pe', I('matmul', ps[ssb][:, :N], lhsT=ones_m[:, :], rhs=sq[:, dc % 2, :N],
                                               start=(dc == 0), stop=(dc == 7)),
                      reads=[('sq', dc % 2), 'ones_m'], writes=[('ps', ssb)])
                self.pending = None

            def finish(self):
                self.flush()
                N, t0 = self.N, self.t0
                rb = rstd_from(self.ssb, N)
                xr = xres(self.blocks)
                for dc in range(8):
                    tc_ = tmp()
                    S.add('dve', I('tensor_tensor', out=T[tc_][:, :N], in0=ybuf[:, dc, :N],
                                                                          in1=rstdT[:, :N], op=ALU.mult),
                          reads=[('ybuf', dc), 'rstd'], writes=[('T', tc_)])
                    S.add('pool' if dc % 2 == 0 else 'dve', I('tensor_tensor', out=xT[:, dc, t0:t0 + N],
                                                                           in0=xT[:, dc, t0:t0 + N],
                                                                           in1=T[tc_][:, :N], op=ALU.add),
                          reads=xres(self.blocks, dc) + [('T', tc_)], writes=xres(self.blocks, dc))

        def lin_fm(bk, wv, wres, KC, rhs_fn, rhs_res_fn, N, M=128):
            for kc in range(KC):
                S.add('pe', I('matmul', ps[bk][0:M, :N], lhsT=wv(kc), rhs=rhs_fn(kc),
                                                      start=(kc == 0), stop=(kc == KC - 1)),
                      reads=[wres] + rhs_res_fn(kc), writes=[('ps', bk)])

        def gelu2(src_ap, res_src, N, out_fn):
            ta = tmp()
            S.add('act', I('activation', out=T[ta][:, :N], in_=src_ap, func=AF.Square,
                                                scale=math.sqrt(0.044715)),
                  reads=res_src, writes=[('T', ta)])
            tb = tmp()
            S.add('dve', I('scalar_tensor_tensor', out=T[tb][:, :N], in0=T[ta][:, :N], scalar=1.0, in1=src_ap,
                                                          op0=ALU.add, op1=ALU.mult),
                  reads=res_src + [('T', ta)], writes=[('T', tb)])
            tcc = tmp()
            S.add('act', I('activation', out=T[tcc][:, :N], in_=T[tb][:, :N], func=AF.Tanh,
                                                scale=math.sqrt(2.0 / math.pi)),
                  reads=[('T', tb)], writes=[('T', tcc)])
            out_fn(tcc)

        for li in range(nL):
            fb = first_blocks[li]
            sbs = [sb_ for sb_ in superblocks(fb) if max_sbs is None or sb_[0] < 4 * max_sbs]
            vb0 = li * NV
            S.add('sp', I('dma_start', out=bcs[:, :], in_=bc_in[li]), writes=['bc'], dma='bc')
            for c in range(4):
                S.add('act', I('activation', out=esb[:, c, :], in_=onesf[:, :], func=AF.Exp, scale=0.0,
                                                         bias=vecs[:, vb0 + 60 + c:vb0 + 61 + c]),
                      reads=['onesf', 'vecs'], writes=['esb'])

            S.add('sp', I('dma_start', out=ybuf[:, :, 0:256], in_=mem_in),
                  writes=[('ybuf', dc) for dc in range(8)], dma='mem')
            ssb = sbank()
            for dc in range(8):
                S.add('act', I('activation', out=sq[:, dc % 2, 0:256], in_=ybuf[:, dc, 0:256], func=AF.Square),
                      reads=[('ybuf', dc)], writes=[('sq', dc % 2)])
                S.add('pe', I('matmul', ps[ssb][:, 0:256], lhsT=ones_m[:, :], rhs=sq[:, dc % 2, 0:256],
                                                               start=(dc == 0), stop=(dc == 7)),
                      reads=[('sq', dc % 2), 'ones_m'], writes=[('ps', ssb)])
            rb = rstd_from(ssb, 256)
            for dc in range(8):
                gc = vb0 + 52 + dc
                S.add('dve', I('scalar_tensor_tensor',
                    out=hT[:, dc, 0:256], in0=ybuf[:, dc, 0:256], scalar=vecs[:, gc:gc + 1], in1=rstdT[:, 0:256],
                    op0=ALU.mult, op1=ALU.mult),
                    reads=[('ybuf', dc), 'rstd', 'vecs'], writes=[('hT', dc)])
            for ci in range(2):
                wt, wres = w_acquire('mk%d' % ci, li)
                wv3 = wt[:, 0:2048].rearrange("p (k m) -> p k m", k=8)
                for hh in range(2):
                    head = ci * 2 + hh
                    bk = bank()
                    lin_fm(bk, lambda kc, hh=hh, wv3=wv3: wv3[:, kc, hh * 128:(hh + 1) * 128], wres, 8,
                           lambda kc: hT[:, kc, 0:256], lambda kc: [('hT', kc)], 256)
                    S.add('act', I('activation', out=kmT[:, head, :], in_=ps[bk][:, 0:256], func=AF.Copy),
                          reads=[('ps', bk)], writes=[('kmT', head)])
                w_release()
            vbk = [bank(), bank()]
            for ci in range(2):
                wt, wres = w_acquire('mv%d' % ci, li)
                wv3 = wt[:, 0:2048].rearrange("p (k m) -> p k m", k=8)
                for mt in range(2):
                    for kc in range(8):
                        S.add('pe', I('matmul',
                            ps[vbk[mt]][:, ci * 256:(ci + 1) * 256], lhsT=hT[:, kc, mt * 128:(mt + 1) * 128],
                            rhs=wv3[:, kc, :], start=(kc == 0), stop=(kc == 7)),
                            reads=[wres, ('hT', kc)], writes=[('ps', vbk[mt])])
                w_release()
            for mt in range(2):
                S.add('dve', I('tensor_copy', out=vmem[:, mt, :], in_=ps[vbk[mt]][:, :]),
                      reads=[('ps', vbk[mt])], writes=[('vmem', mt)])

            for si, (b0, nb, kvonly) in enumerate(sbs):
                N = nb * 128
                t0 = b0 * 128
                blocks = list(range(b0, b0 + nb))
                xr = xres(blocks)

                norm_in(li, 0, blocks, t0, N)
                hres = lambda kc: [('hT', kc)]
                hrhs = lambda kc: hT[:, kc, :N]
                qB, uB, vbB, yaB, ybB, ycB, plB = B[0], B[1], B[2], B[3], B[4], B[5], B[6]
                for ci in (() if kvonly else range(2)):
                    wt, wres = w_acquire('q%d' % ci, li)
                    wv3 = wt[:, 0:2048].rearrange("p (k m) -> p k m", k=8)
                    for hh in range(2):
                        c = ci * 2 + hh
                        bk = bank()
                        lin_fm(bk, lambda kc, hh=hh, wv3=wv3: wv3[:, kc, hh * 128:(hh + 1) * 128], wres, 8, hrhs, hres, N)
                        S.add('act', I('activation', out=qB[:, c, :N], in_=ps[bk][:, :N], func=AF.Copy),
                              reads=[('ps', bk)], writes=[('B0', c)])
                    w_release()
                wt, wres = w_acquire('k', li)
                wv3 = wt[:, 0:2048].rearrange("p (k m) -> p k m", k=8)
                for g in range(2):
                    bk = bank()
                    lin_fm(bk, lambda kc, g=g, wv3=wv3: wv3[:, kc, g * 128:(g + 1) * 128], wres, 8, hrhs, hres, N)
                    for i, b in enumerate(blocks):
                        s5 = b % 5
                        S.add('dve', I('tensor_copy',
                            out=kTe[0:64, g, s5 * 128:(s5 + 1) * 128], in_=ps[bk][0:64, i * 128:(i + 1) * 128]),
                            reads=[('ps', bk)], writes=[('kT', g, s5)])
                        S.add('dve', I('tensor_copy',
                            out=kTo[64:128, g, s5 * 128:(s5 + 1) * 128], in_=ps[bk][64:128, i * 128:(i + 1) * 128]),
                            reads=[('ps', bk)], writes=[('kT', g, s5)])
                w_release()
                wt, wres = w_acquire('v', li)
                wv3 = wt[:, 0:1024].rearrange("p (k m) -> p k m", k=8)
                bk = bank()
                for i, b in enumerate(blocks):
                    for kc in range(8):
                        S.add('pe', I('matmul',
                            ps[bk][:, i * 128:(i + 1) * 128], lhsT=hT[:, kc, i * 128:(i + 1) * 128], rhs=wv3[:, kc, :],
                            start=(kc == 0), stop=(kc == 7)),
                            reads=[wres, ('hT', kc)], writes=[('ps', bk)])
                for i, b in enumerate(blocks):
                    s5 = b % 5
                    S.add('dve', I('tensor_scalar',
                        out=v_sb[:, s5, :], in0=ps[bk][:, i * 128:(i + 1) * 128], scalar1=valid[:, b:b + 1], scalar2=None,
                        op0=ALU.mult),
                        reads=[('ps', bk), 'valid'], writes=[('v', s5)])
                w_release()
                for ci in (() if kvonly else range(2)):
                    wt, wres = w_acquire('su%d' % ci, li)
                    wv3 = wt[:, 0:2048].rearrange("p (k m) -> p k m", k=8)
                    for hh in range(2):
                        c = ci * 2 + hh
                        bk = bank()
                        lin_fm(bk, lambda kc, hh=hh, wv3=wv3: wv3[:, kc, hh * 128:(hh + 1) * 128], wres, 8, hrhs, hres, N)

                        def fin(t3, bk=bk, c=c):
                            S.add('dve', I('scalar_tensor_tensor', out=uB[:, c, :N], in0=T[t3][:, :N], scalar=1.0,
                                                                          in1=ps[bk][:, :N], op0=ALU.add, op1=ALU.mult),
                                  reads=[('ps', bk), ('T', t3)], writes=[('B1', c)])
                        gelu2(ps[bk][:, :N], [('ps', bk)], N, fin)
                    w_release()
                svb = [] if kvonly else [bank() for _ in range(nb)]
                for ci in (() if kvonly else range(2)):
                    wt, wres = w_acquire('sv%d' % ci, li)
                    wv3 = wt[:, 0:2048].rearrange("p (k m) -> p k m", k=8)
                    for i in range(nb):
                        for kc in range(8):
                            S.add('pe', I('matmul',
                                ps[svb[i]][:, ci * 256:(ci + 1) * 256], lhsT=hT[:, kc, i * 128:(i + 1) * 128],
                                rhs=wv3[:, kc, :], start=(kc == 0), stop=(kc == 7)),
                                reads=[wres, ('hT', kc)], writes=[('ps', svb[i])])
                    w_release()
                for i in (() if kvonly else range(nb)):
                    bk = svb[i]

                    def fin(t3, bk=bk, i=i):
                        tg = tmp()
                        S.add('dve', I('scalar_tensor_tensor', out=T[tg][:, :], in0=T[t3][:, :], scalar=1.0,
                                                                      in1=ps[bk][:, :], op0=ALU.add, op1=ALU.mult),
                              reads=[('ps', bk), ('T', t3)], writes=[('T', tg)])
                        tj = tmp()
                        S.add('dve', I('scalar_tensor_tensor', out=T[tj][:, :], in0=T[tg][:, :], scalar=1.0, in1=T[tg][:, :],
                                                                      op0=ALU.mult, op1=ALU.mult,
                                                                      accum_out=small[:, 0:1]),
                              reads=[('T', tg)], writes=[('T', tj), ('small', 0)])
                        S.add('dve', I('tensor_scalar', out=small[:, 1:2], in0=small[:, 0:1], scalar1=1.0 / 512.0,
                                                               scalar2=4.0 * EPS, op0=ALU.mult, op1=ALU.add),
                              reads=[('small', 0)], writes=[('small', 1)])
                        S.add('pool', I('tensor_tensor', out=small[:, 2:3], in0=small[:, 1:2], in1=negh[:, 0:1],
                                                                op=ALU.pow),
                              reads=[('small', 1), 'negh'], writes=[('small', 2)])
                        S.add('dve', I('scalar_tensor_tensor', out=vbB[:, i, :], in0=T[tg][:, :],
                                                                      scalar=small[:, 2:3], in1=bcs[:, 0:512],
                                                                      op0=ALU.mult, op1=ALU.mult),
                              reads=[('T', tg), ('small', 2), 'bc'], writes=[('B2', i)])
                    gelu2(ps[bk][:, :], [('ps', bk)], 512, fin)
                pcb = [bank() for _ in range(nb)]
                for ci in range(2):
                    wt, wres = w_acquire('pc%d' % ci, li)
                    wv3 = wt[:, 0:2048].rearrange("p (k m) -> p k m", k=8)
                    for i in range(nb):
                        for kc in range(8):
                            S.add('pe', I('matmul',
                                ps[pcb[i]][:, ci * 256:(ci + 1) * 256], lhsT=hT[:, kc, i * 128:(i + 1) * 128],
                                rhs=wv3[:, kc, :], start=(kc == 0), stop=(kc == 7)),
                                reads=[wres, ('hT', kc)], writes=[('ps', pcb[i])])
                    w_release()
                for i, b in enumerate(blocks):
                    s5 = b % 5
                    S.add('dve', I('tensor_scalar',
                        out=c_sb[:, s5, :], in0=ps[pcb[i]][:, :], scalar1=valid[:, b:b + 1], scalar2=None, op0=ALU.mult),
                        reads=[('ps', pcb[i]), 'valid'], writes=[('c', s5)])

                if kvonly:
                    continue

                items = [(i, g) for i in range(nb) for g in range(2)]

                def att_scores(it, n_it):
                    i, g = it
                    b = blocks[i]
                    kts = ([b - 1] if b > fb else []) + [b]
                    dbl = n_it % 2
                    sbk = []
                    for ki, kt in enumerate(kts):
                        own = (kt == b)
                        bk = bank()
                        sbk.append(bk)
                        s5 = kt % 5
                        for par in range(2):
                            S.add('pe', I('matmul',
                                ps[bk][:, par * 256:(par + 1) * 256],
                                lhsT=(kTe if par == 0 else kTo)[:, g, s5 * 128:(s5 + 1) * 128],
                                rhs=qB[:, 2 * g:2 * g + 2, i * 128:(i + 1) * 128],
                                start=True, stop=True),
                                reads=[('kT', g, s5), ('B0', 2 * g), ('B0', 2 * g + 1)], writes=[('ps', bk)])
                        pidx = dbl * 2 + (1 if own else 0)
                        pv = ptb[:, pidx, :].rearrange("p (a q) -> p a q", a=4)
                        sv_ = ps[bk][:, :].rearrange("p (a q) -> p a q", a=4)
                        if own:
                            full, part, qlo = slice(0, 64), slice(64, 128), 64
                        else:
                            full, part, qlo = slice(64, 128), slice(0, 64), 0
                        S.add('act', I('activation',
                            out=pv[full, :, :], in_=sv_[full, :, :], func=AF.Exp, scale=0.125),
                            reads=[('ps', bk)], writes=[('ptb', pidx)])
                        S.add('act', I('activation',
                            out=pv[part, :, qlo:qlo + 64], in_=sv_[part, :, qlo:qlo + 64], func=AF.Exp, scale=0.125),
                            reads=[('ps', bk)], writes=[('ptb', pidx)])
                    return (i, g, b, kts, dbl)

                def att_pv(stt):
                    i, g, b, kts, dbl = stt
                    ob = bank()
                    for which in range(2):
                        for par in range(2):
                            for ki, kt in enumerate(kts):
                                own = (kt == b)
                                pidx = dbl * 2 + (1 if own else 0)
                                s5 = kt % 5
                                if which == 0:
                                    lhs = v_sb[:, s5, g * 64:(g + 1) * 64]
                                    lres = ('v', s5)
                                else:
                                    lhs = vrep[:, kt, :]
                                    lres = ('vrep', kt)
                                S.add('pe', I('matmul',
                                    ps[ob][par * 64:(par + 1) * 64, which * 256:(which + 1) * 256], lhsT=lhs,
                                    rhs=ptb[:, pidx, par * 256:(par + 1) * 256], start=(ki == 0), stop=(ki == len(kts) - 1)),
                                    reads=[lres, ('ptb', pidx)], writes=[('ps', ob)])
                    ta = tmp()
                    r3 = lambda ap: ap.rearrange("p (a q) -> p a q", a=2)
                    S.add('dve', I('tensor_tensor', out=r3(T[ta][:, 0:256]), in0=r3(ps[ob][:, 256:512]),
                                                           in1=esb[:, 2 * g:2 * g + 2, :], op=ALU.add),
                          reads=[('ps', ob), 'esb'], writes=[('T', ta)])
                    tb = tmp()
                    S.add('dve', I('reciprocal', out=T[tb][:, 0:256], in_=T[ta][:, 0:256]),
                          reads=[('T', ta)], writes=[('T', tb)])
                    S.add('dve', I('tensor_tensor', out=yaB[:, 2 * g:2 * g + 2, i * 128:(i + 1) * 128],
                                                           in0=r3(ps[ob][:, 0:256]), in1=r3(T[tb][:, 0:256]), op=ALU.mult),
                          reads=[('ps', ob), ('T', tb)], writes=[('B3', 2 * g), ('B3', 2 * g + 1)])

                prev = None
                for n_it, it in enumerate(items):
                    cur = att_scores(it, n_it)
                    if prev is not None:
                        att_pv(prev)
                    prev = cur
                att_pv(prev)

                wt, wres = w_acquire('ws', li)
                ws3 = wt[:, 0:512].rearrange("p (g i) -> p g i", g=4)
                for i in range(nb):
                    bk = bank()
                    for g in range(4):
                        S.add('pe', I('matmul',
                            ps[bk][:, g * 128:g * 128 + 64], lhsT=vbB[0:64, i, g * 128:(g + 1) * 128], rhs=ws3[0:64, g, 0:64],
                            start=True, stop=True),
                            reads=[wres, ('B2', i)], writes=[('ps', bk)])
                        S.add('pe', I('matmul',
                            ps[bk][:, g * 128 + 64:(g + 1) * 128], lhsT=vbB[:, i, g * 128:(g + 1) * 128], rhs=ws3[:, g, 64:128],
                            start=True, stop=True),
                            reads=[wres, ('B2', i)], writes=[('ps', bk)])
                    ta = tmp()
                    S.add('dve', I('tensor_tensor', out=T[ta][:, :], in0=ps[bk][:, :], in1=bcs[:, 512:1024],
                                                                        op=ALU.add),
                          reads=[('ps', bk), 'bc'], writes=[('T', ta)])
                    S.add('dve', I('scalar_tensor_tensor',
                        out=ybB[:, :, i * 128:(i + 1) * 128], in0=uB[:, :, i * 128:(i + 1) * 128], scalar=0.5,
                        in1=T[ta][:, :].rearrange("p (g q) -> p g q", g=4), op0=ALU.mult, op1=ALU.mult),
                        reads=[('T', ta)] + [('B1', c) for c in range(4)], writes=[('B4', c) for c in range(4)])
                w_release()

                for i, b in enumerate(blocks):
                    bk = bank()
                    var = 1 if b == 4 else 0
                    hasprev = b > fb
                    for g in range(4):
                        S.add('pe', I('matmul',
                            ps[bk][:, g * 128:(g + 1) * 128], lhsT=c_sb[:, b % 5, g * 128:(g + 1) * 128],
                            rhs=ptm_v[:, var, g, 0, :], start=True, stop=(not hasprev)),
                            reads=[('c', b % 5), 'ptm'], writes=[('ps', bk)])
                        if hasprev:
                            S.add('pe', I('matmul',
                                ps[bk][:, g * 128:(g + 1) * 128], lhsT=c_sb[:, (b - 1) % 5, g * 128:(g + 1) * 128],
                                rhs=ptm_v[:, var, g, 1, :], start=False, stop=True),
                                reads=[('c', (b - 1) % 5), 'ptm'], writes=[('ps', bk)])
                    S.add('act', I('activation',
                        out=plB[:, :, i * 128:(i + 1) * 128], in_=ps[bk][:, :].rearrange("p (g q) -> p g q", g=4), func=AF.Copy),
                        reads=[('ps', bk)], writes=[('B6', c) for c in range(4)])
                wt, wres = w_acquire('wp', li)
                wp3 = wt[:, 0:512].rearrange("p (g d) -> p g d", g=4)
                for g in range(4):
                    bk = bank()
                    S.add('pe', I('matmul', ps[bk][:, :N], lhsT=wp3[:, g, :], rhs=plB[:, g, :N],
                                                               start=True, stop=True),
                          reads=[wres, ('B6', g)], writes=[('ps', bk)])
                    S.add('dve', I('tensor_scalar', out=ycB[:, g, :N], in0=ps[bk][:, :N],
                                                                      scalar1=vecs[:, vb0 + 48 + g:vb0 + 49 + g], scalar2=None,
                                                                      op0=ALU.mult),
                          reads=[('ps', bk), 'vecs'], writes=[('B5', g)])
                w_release()

                brs = [(yaB, 'B3'), (ybB, 'B4'), (ycB, 'B5')]
                for dc in range(8):
                    mB, mname, mc = (B[0], 'B0', dc) if dc < 4 else (B[1], 'B1', dc - 4)
                    acc = None
                    for n in range(3):
                        wt, wres = w_acquire('gb%d_%d' % (dc, n), li)
                        gv = wt[:, 0:1024].rearrange("p (k m) -> p k m", k=8)
                        bv = wt[:, 1024:1536].rearrange("p (k m) -> p k m", k=4)
                        gk = bank()
                        lin_fm(gk, lambda kc, gv=gv: gv[:, kc, :], wres, 8, hrhs, hres, N)
                        tth = tmp()
                        S.add('act', I('activation', out=T[tth][:, :N], in_=ps[gk][:, :N], func=AF.Tanh,
                                                                           scale=0.5),
                              reads=[('ps', gk)], writes=[('T', tth)])
                        pk = bank()
                        brB, brn = brs[n]
                        lin_fm(pk, lambda kc, bv=bv: bv[:, kc, :], wres, 4, lambda kc, brB=brB: brB[:, kc, :N],
                               lambda kc, brn=brn: [(brn, kc)], N)
                        w_release()
                        tp = tmp()
                        S.add('dve', I('scalar_tensor_tensor',
                            out=T[tp][:, :N], in0=T[tth][:, :N], scalar=1.0, in1=ps[pk][:, :N], op0=ALU.add, op1=ALU.mult),
                            reads=[('T', tth), ('ps', pk)], writes=[('T', tp)])
                        if n == 0:
                            acc = tp
                        elif n == 1:
                            ta2 = tmp()
                            S.add('pool', I('tensor_tensor',
                                out=T[ta2][:, :N], in0=T[acc][:, :N], in1=T[tp][:, :N], op=ALU.add),
                                reads=[('T', acc), ('T', tp)], writes=[('T', ta2)])
                            acc = ta2
                        else:
                            S.add('pool', I('tensor_tensor',
                                out=mB[:, mc, :N], in0=T[acc][:, :N], in1=T[tp][:, :N], op=ALU.add),
                                reads=[('T', acc), ('T', tp)], writes=[(mname, mc)])

                ep = Epilogue(li, 1, blocks, t0, N, 0.5)
                mrhs = lambda kc: (B[0][:, kc, :N] if kc < 4 else B[1][:, kc - 4, :N])
                mres = lambda kc: [('B0', kc)] if kc < 4 else [('B1', kc - 4)]
                for ci in range(4):
                    wt, wres = w_acquire('wo%d' % ci, li)
                    wv3 = wt[:, 0:2048].rearrange("p (k m) -> p k m", k=8)
                    for hh in range(2):
                        dc = ci * 2 + hh
                        bk = bank()
                        lin_fm(bk, lambda kc, hh=hh, wv3=wv3: wv3[:, kc, hh * 128:(hh + 1) * 128], wres, 8, mrhs, mres, N)
                        ep.push(dc, bk)
                    w_release()
                ep.finish()

                norm_in(li, 2, blocks, t0, N)
                qmB, pmB, omB = B[2], B[6], B[7]
                for ci in range(2):
                    wt, wres = w_acquire('wq%d' % ci, li)
                    wv3 = wt[:, 0:2048].rearrange("p (k m) -> p k m", k=8)
                    for hh in range(2):
                        head = ci * 2 + hh
                        bk = bank()
                        lin_fm(bk, lambda kc, hh=hh, wv3=wv3: wv3[:, kc, hh * 128:(hh + 1) * 128], wres, 8, hrhs, hres, N)
                        S.add('act', I('activation', out=qmB[:, head, :N], in_=ps[bk][:, :N], func=AF.Copy),
                              reads=[('ps', bk)], writes=[('B2', head)])
                    w_release()
                mscale = 1.0 / math.sqrt(128.0)

                def mem_scores(head):
                    for mt in range(2):
                        bk = bank()
                        pi_ = (head % 2) * 2 + mt
                        S.add('pe', I('matmul', ps[bk][:, :N], lhsT=kmT[:, head, mt * 128:(mt + 1) * 128],
                                                                     rhs=qmB[:, head, :N], start=True, stop=True),
                              reads=[('kmT', head), ('B2', head)], writes=[('ps', bk)])
                        S.add('act', I('activation', out=pmB[:, pi_, :N], in_=ps[bk][:, :N], func=AF.Exp,
                                                                           scale=mscale),
                              reads=[('ps', bk)], writes=[('B6', pi_)])

                def mem_pv(head):
                    ob, db = bank(), bank()
                    for mt in range(2):
                        pi_ = (head % 2) * 2 + mt
                        S.add('pe', I('matmul', ps[ob][:, :N], lhsT=vmem[:, mt, head * 128:(head + 1) * 128],
                                                                      rhs=pmB[:, pi_, :N], start=(mt == 0), stop=(mt == 1)),
                              reads=[('vmem', mt), ('B6', pi_)], writes=[('ps', ob)])
                    for mt in range(2):
                        pi_ = (head % 2) * 2 + mt
                        S.add('pe', I('matmul', ps[db][:, :N], lhsT=ones_1[:, :], rhs=pmB[:, pi_, :N],
                                                                      start=(mt == 0), stop=(mt == 1)),
                              reads=['ones_1', ('B6', pi_)], writes=[('ps', db)])
                    ta = tmp()
                    S.add('dve', I('reciprocal', out=T[ta][:, :N], in_=ps[db][:, :N]),
                          reads=[('ps', db)], writes=[('T', ta)])
                    S.add('dve', I('tensor_tensor', out=omB[:, head, :N], in0=ps[ob][:, :N], in1=T[ta][:, :N], op=ALU.mult),
                          reads=[('ps', ob), ('T', ta)], writes=[('B7', head)])

                mem_scores(0)
                for head in range(4):
                    if head + 1 < 4:
                        mem_scores(head + 1)
                    mem_pv(head)
                ep = Epilogue(li, 3, blocks, t0, N, 1.0)
                for ci in range(2):
                    wt, wres = w_acquire('wom%d' % ci, li)
                    wv3 = wt[:, 0:2048].rearrange("p (k m) -> p k m", k=4)
                    for hh in range(4):
                        dc = ci * 4 + hh
                        bk = bank()
                        lin_fm(bk, lambda kc, hh=hh, wv3=wv3: wv3[:, kc, hh * 128:(hh + 1) * 128], wres, 4,
                               lambda kc: omB[:, kc, :N], lambda kc: [('B7', kc)], N)
                        ep.push(dc, bk)
                    w_release()
                ep.finish()

                norm_in(li, 4, blocks, t0, N)
                for ci in range(16):
                    wt, wres = w_acquire('up%d' % ci, li)
                    wv3 = wt[:, 0:2048].rearrange("p (k m) -> p k m", k=8)
                    for hh in range(2):
                        j = ci * 2 + hh
                        bk = bank()
                        lin_fm(bk, lambda kc, hh=hh, wv3=wv3: wv3[:, kc, hh * 128:(hh + 1) * 128], wres, 8, hrhs, hres, N)
                        tr = tmp()
                        S.add('act', I('activation', out=T[tr][:, :N], in_=ps[bk][:, :N], func=AF.Relu),
                              reads=[('ps', bk)], writes=[('T', tr)])
                        S.add('dve', I('tensor_tensor', out=B[j // 4][:, j % 4, :N], in0=T[tr][:, :N],
                                                                                 in1=ps[bk][:, :N], op=ALU.mult),
                              reads=[('ps', bk), ('T', tr)], writes=[('B%d' % (j // 4), j % 4)])
                    w_release()
                ep = Epilogue(li, 5, blocks, t0, N, 1.0)
                for dc in range(8):
                    bk = bank()
                    for half in range(2):
                        wt, wres = w_acquire('dn%d%s' % (dc, 'ab'[half]), li)
                        wv3 = wt[:, 0:2048].rearrange("p (k m) -> p k m", k=16)
                        for kk in range(16):
                            j = half * 16 + kk
                            S.add('pe', I('matmul',
                                ps[bk][:, :N], lhsT=wv3[:, kk, :], rhs=B[j // 4][:, j % 4, :N],
                                start=(j == 0), stop=(j == 31)),
                                reads=[wres, ('B%d' % (j // 4), j % 4)], writes=[('ps', bk)])
                        w_release()
                    ep.push(dc, bk)
                ep.finish()

                if li == nL - 1 and b0 >= 4:
                    o0 = t0 - HALO
                    S.add('sp', I('dma_start', out=out_d[:, :, o0:o0 + N], in_=xT[:, :, t0:t0 + N]),
                          reads=xr, writes=[('out', b0)], dma='st')

        S.add('sp', I('nop', ), reads=[('out', b) for b in (4, 8, 12, 16)][:(None if max_sbs is None else max(max_sbs - 1, 0))])
        assert wstate['acq'] == len(seq), (wstate, len(seq))
        S.emit(nc)
    return nc


def superblocks(fb):
    sbs = [(fb, 1, True)]
    if fb + 1 < 4:
        sbs.append((fb + 1, 4 - (fb + 1), False))
    for k in range(1, 5):
        sbs.append((4 * k, 4, False))
    return sbs


KV_CHUNKS = ('k', 'v', 'pc0', 'pc1')


def make_ptm(first_variant_is_seq_start):
    P = np.zeros((128, 2, 4, 2, 128), np.float32)
    s = np.arange(128)[:, None]
    t = np.arange(128)[None, :]
    for g, w in enumerate((2, 4, 8, 16)):
        own = ((s <= t) & (s > t - w)).astype(np.float32) / w - (s == t).astype(np.float32)
        prv = ((s + 0) > (128 + t - w)).astype(np.float32) / w
        P[:, 0, g, 0, :] = own
        P[:, 0, g, 1, :] = prv
        if first_variant_is_seq_start:
            cnt = np.minimum(t + 1, w).astype(np.float32)
            P[:, 1, g, 0, :] = ((s <= t) & (s > t - w)).astype(np.float32) / cnt - (s == t).astype(np.float32)
            P[:, 1, g, 1, :] = 0.0
        else:
            P[:, 1, g, 0, :] = own
            P[:, 1, g, 1, :] = prv
    return P.reshape(128, -1)


def make_vecs(layers, g_norm, g_mem, pool_scale, attn_sinks):
    nL = len(layers)
    v = np.zeros((128, nL * NV), np.float32)
    for i, l in enumerate(layers):
        base = i * NV
        for gi in range(6):
            v[:, base + gi * 8: base + gi * 8 + 8] = g_norm[l, gi].reshape(8, 128).T
        v[:, base + 48: base + 52] = pool_scale[l].reshape(4, 128).T
        v[:, base + 52: base + 60] = g_mem[l].reshape(8, 128).T
        for c in range(4):
            v[0:64, base + 60 + c] = attn_sinks[l, 2 * c]
            v[64:128, base + 60 + c] = attn_sinks[l, 2 * c + 1]
    return v


def make_bc(layers, g_sgu, b_spatial):
    nL = len(layers)
    bc = np.zeros((nL, 128, NBC), np.float32)
    for i, l in enumerate(layers):
        bc[i, :, 0:512] = g_sgu[l][None, :]
        bc[i, :, 512:1024] = b_spatial[l].reshape(1, 512)
    return bc


_PROG_CACHE = {}


def _get_prog(nL, first_blocks):
    key = (nL, tuple(first_blocks))
    if key not in _PROG_CACHE:
        _PROG_CACHE[key] = build_program(list(range(nL)), list(first_blocks))
    return _PROG_CACHE[key]


FUSED = True


def kernel(x, mem, g_norm, g_mem, w_in, attn_sinks, w_spatial, b_spatial, g_sgu, w_pool, pool_scale,
           w_branch, w_out, w_q_mem, w_kv_mem, w_o_mem, w_up, w_down):
    f = lambda a: np.ascontiguousarray(np.asarray(a, dtype=np.float32))
    x, mem, g_norm, g_mem, w_in, attn_sinks = f(x), f(mem), f(g_norm), f(g_mem), f(w_in), f(attn_sinks)
    w_spatial, b_spatial, g_sgu, w_pool, pool_scale = f(w_spatial), f(b_spatial), f(g_sgu), f(w_pool), f(pool_scale)
    w_branch, w_out, w_q_mem, w_kv_mem, w_o_mem, w_up, w_down = (f(w_branch), f(w_out), f(w_q_mem), f(w_kv_mem),
                                                                 f(w_o_mem), f(w_up), f(w_down))
    wl = [pack_layer_weights(l, w_in, w_spatial, w_pool, w_branch, w_out, w_q_mem, w_kv_mem, w_o_mem, w_up, w_down)
          for l in range(DEPTH)]

    def core_consts(c):
        half = c % 2
        valid = np.ones((128, NBLK), np.float32)
        if half == 0:
            valid[:, 0:4] = 0.0
        return valid, make_ptm(half == 0)

    def x_layout(xcur):
        outs = []
        for c in range(NCORES):
            b, half = c // 2, c % 2
            start = half * TOK - HALO
            xs = np.zeros((TT, D), np.float32)
            lo = max(start, 0)
            xs[lo - start:, :] = xcur[b, lo:start + TT, :]
            outs.append(np.ascontiguousarray(xs.reshape(TT, 8, 128).transpose(2, 1, 0)))
        return outs

    memT = [np.ascontiguousarray(mem[c // 2].reshape(256, 8, 128).transpose(2, 1, 0)) for c in range(NCORES)]
    consts = [core_consts(c) for c in range(NCORES)]

    def run(layers, first_blocks, xcur):
        nc = _get_prog(len(layers), first_blocks)
        wst = np.stack([wl[l] for l in layers], axis=0)
        vecs = make_vecs(layers, g_norm, g_mem, pool_scale, attn_sinks)
        bc = make_bc(layers, g_sgu, b_spatial)
        xs = x_layout(xcur)
        in_maps = []
        for c in range(NCORES):
            in_maps.append({"xT_in": xs[c], "memT": memT[c], "wst": wst, "vecs": vecs, "bc": bc,
                            "valid": consts[c][0], "ptm": consts[c][1]})
        res = run_bass_kernel_spmd(nc, in_maps, core_ids=list(range(NCORES)))
        out = np.zeros((BATCH, SEQ, D), np.float32)
        for c in range(NCORES):
            b, half = c // 2, c % 2
            o = np.asarray(res.results[c]["outT"])
            out[b, half * TOK:(half + 1) * TOK, :] = o.transpose(2, 1, 0).reshape(TOK, D)
        return out

    if FUSED:
        return run(list(range(DEPTH)), [0, 1, 2, 3], x)
    xcur = x
    for l in range(DEPTH):
        xcur = run([l], [3], xcur)
    return xcur
```
